# Optimizing a Trainium2 kernel written in Bass

```python
import math
import jax, jax.numpy as jnp
from jax import lax
import numpy as np


D_MODEL = 4096
BATCH = 8
SEQ = 2048
DEPTH = 2
DEC_BATCH = 4
DEC_SEQ = 4096
PAST_LEN = 128

A_GROUPS = 4
A_GROUP_DIM = D_MODEL // 8
A_WIDTH = A_GROUPS * A_GROUP_DIM
B_HEADS = 4
B_DV = D_MODEL // 8
B_DK = B_DV // 2
B_QK_WIDTH = B_HEADS * B_DK
B_V_WIDTH = B_HEADS * B_DV
GATE_RANK = 16
GATE_TEMP = 16.0
GLA_CHUNK = 64
AB_IN = A_WIDTH + 2 * B_QK_WIDTH + 2 * B_V_WIDTH + 2 * GATE_RANK
AB_MIX = A_WIDTH + B_V_WIDTH
C_HEAD_DIM = 64
C_HEADS = D_MODEL // C_HEAD_DIM
C_KV_HEADS = 8
C_IN = (C_HEADS + 2 * C_KV_HEADS) * C_HEAD_DIM
WINDOW = 128
ATTN_BLOCK = 128
N_BUCKETS = 32
MAX_DISTANCE = 128
D_FF = -(-8 * D_MODEL // (3 * 256)) * 256
N_EVEN = (DEPTH + 1) // 2
N_ODD = DEPTH // 2
DEEPNORM_ALPHA = (2 * DEPTH) ** 0.25
DEEPNORM_BETA = (8 * DEPTH) ** -0.25
LN_EPS = 1e-5

kernel_name = 'hybrid_fnet_gla_swa_encoder'


def _standardize(xf):
    mu = jnp.mean(xf, axis=-1, keepdims=True)
    xc = xf - mu
    return xc * lax.rsqrt(jnp.mean(xc * xc, axis=-1, keepdims=True) + LN_EPS)


def layer_norm(x, g, b):
    y = _standardize(x.astype(jnp.float32)) * g.astype(jnp.float32) + b.astype(jnp.float32)
    return y.astype(x.dtype)


def group_norm(xf, g):
    return _standardize(xf.astype(jnp.float32)) * g.astype(jnp.float32)


def t5_bucket(rel):
    nb = N_BUCKETS // 2
    max_exact = nb // 2
    ret = jnp.where(rel > 0, nb, 0)
    n = jnp.abs(rel)
    nf = jnp.maximum(n, 1).astype(jnp.float32)
    large = max_exact + (jnp.log(nf / max_exact) / math.log(MAX_DISTANCE / max_exact)
                         * (nb - max_exact)).astype(jnp.int32)
    large = jnp.minimum(large, nb - 1)
    return ret + jnp.where(n < max_exact, n, large)


def band_bias(table):
    qq = jnp.arange(ATTN_BLOCK)[:, None]
    kk = jnp.arange(3 * ATTN_BLOCK)[None, :]
    rel = kk - ATTN_BLOCK - qq
    return jnp.transpose(table[t5_bucket(rel)], (2, 0, 1)).astype(jnp.float32)


def gla_chunked(q, k, v, log_a):
    B, S, H, dk = q.shape
    dv = v.shape[-1]
    nc = S // GLA_CHUNK

    def to_chunks(t):
        return t.reshape(B, nc, GLA_CHUNK, H, t.shape[-1]).transpose(1, 0, 3, 2, 4)

    qc, kc, vc = to_chunks(q), to_chunks(k), to_chunks(v)
    gc = jnp.cumsum(to_chunks(log_a), axis=3)
    tril = jnp.tril(jnp.ones((GLA_CHUNK, GLA_CHUNK), dtype=bool))

    def step(state, inp):
        q_, k_, v_, g_ = inp
        diff = g_[:, :, :, None, :] - g_[:, :, None, :, :]
        decay = jnp.exp(jnp.where(tril[:, :, None], diff, -jnp.inf))
        scores = jnp.einsum('bhid,bhijd->bhij', q_, decay * k_[:, :, None, :, :])
        o_intra = jnp.einsum('bhij,bhje->bhie', scores, v_)
        o_inter = jnp.einsum('bhid,bhde->bhie', q_ * jnp.exp(g_), state)
        g_last = g_[:, :, -1:, :]
        k_dec = k_ * jnp.exp(g_last - g_)
        state = state * jnp.swapaxes(jnp.exp(g_last), -1, -2) + jnp.einsum('bhjd,bhje->bhde', k_dec, v_)
        return state, o_intra + o_inter

    state0 = jnp.zeros((B, H, dk, dv), jnp.float32)
    _, ys = lax.scan(step, state0, (qc, kc, vc, gc))
    return ys.transpose(1, 0, 3, 2, 4).reshape(B, S, H, dv)


def mixer_ab(x, w_in, fourier_g, gate_w2, gate_b, head_norm_g, w_out):
    B, S, _ = x.shape
    proj = x @ w_in
    sizes = [A_WIDTH, B_QK_WIDTH, B_QK_WIDTH, B_V_WIDTH, B_V_WIDTH, GATE_RANK, GATE_RANK]
    cuts = np.cumsum(sizes)[:-1].tolist()
    u, q, k, v, r, g_fwd, g_bwd = jnp.split(proj, cuts, axis=-1)

    u = group_norm(u.reshape(B, S, A_GROUPS, A_GROUP_DIM), fourier_g)
    a_out = jnp.real(jnp.fft.fft2(u, axes=(1, 3), norm='ortho')).reshape(B, S, A_WIDTH)

    qf = q.astype(jnp.float32).reshape(B, S, B_HEADS, B_DK) * (B_DK ** -0.5)
    kf = k.astype(jnp.float32).reshape(B, S, B_HEADS, B_DK)
    vf = v.astype(jnp.float32).reshape(B, S, B_HEADS, B_DV)

    def log_gate(lr, d):
        z = lr.astype(jnp.float32) @ gate_w2[d].astype(jnp.float32) + gate_b[d].astype(jnp.float32)
        return (jax.nn.log_sigmoid(z) / GATE_TEMP).reshape(B, S, B_HEADS, B_DK)

    la_f = log_gate(g_fwd, 0)
    la_b = log_gate(g_bwd, 1)
    o_f = gla_chunked(qf, kf, vf, la_f)
    o_b = jnp.flip(gla_chunked(jnp.flip(qf, 1), jnp.flip(kf, 1), jnp.flip(vf, 1), jnp.flip(la_b, 1)), 1)
    o = group_norm(o_f + o_b, head_norm_g)
    o = o * jax.nn.silu(r.astype(jnp.float32).reshape(B, S, B_HEADS, B_DV))
    b_out = o.reshape(B, S, B_V_WIDTH)

    mix = jnp.concatenate([a_out, b_out], axis=-1).astype(x.dtype)
    return mix @ w_out


def window_attention(q, k, v, sinks, bias):
    B, S, Hq, dh = q.shape
    Hkv = k.shape[2]
    G = Hq // Hkv
    nb = S // ATTN_BLOCK
    scale = dh ** -0.5
    q = q.astype(jnp.float32)
    k = k.astype(jnp.float32)
    v = v.astype(jnp.float32)

    qb = q.reshape(B, nb, ATTN_BLOCK, Hkv, G, dh).transpose(1, 0, 3, 4, 2, 5)

    def key_blocks(t):
        tp = jnp.pad(t, ((0, 0), (ATTN_BLOCK, ATTN_BLOCK), (0, 0), (0, 0)))
        tp = tp.reshape(B, nb + 2, ATTN_BLOCK, Hkv, dh)
        tb = jnp.concatenate([tp[:, :-2], tp[:, 1:-1], tp[:, 2:]], axis=2)
        return tb.transpose(1, 0, 3, 2, 4)

    kb, vb = key_blocks(k), key_blocks(v)
    rel = jnp.arange(3 * ATTN_BLOCK)[None, :] - ATTN_BLOCK - jnp.arange(ATTN_BLOCK)[:, None]
    band = jnp.abs(rel) <= WINDOW
    bias_g = bias.reshape(Hkv, G, ATTN_BLOCK, 3 * ATTN_BLOCK)
    sink = sinks.astype(jnp.float32).reshape(Hkv, G)[None, :, :, None, None]

    def one_block(args):
        n, qn, kn, vn = args
        s = jnp.einsum('bkgqd,bkjd->bkgqj', qn, kn) * scale + bias_g
        kpos = n * ATTN_BLOCK - ATTN_BLOCK + jnp.arange(3 * ATTN_BLOCK)
        ok = band & ((kpos >= 0) & (kpos < S))[None, :]
        s = jnp.where(ok, s, -jnp.inf)
        m = jnp.maximum(jnp.max(s, axis=-1, keepdims=True), sink)
        p = jnp.exp(s - m)
        denom = jnp.sum(p, axis=-1, keepdims=True) + jnp.exp(sink - m)
        return jnp.einsum('bkgqj,bkjd->bkgqd', p, vn) / denom

    out = lax.map(one_block, (jnp.arange(nb), qb, kb, vb))
    return out.transpose(1, 0, 4, 2, 3, 5).reshape(B, S, Hq * dh)


def mixer_c(x, w_in, sinks, w_out, bias):
    B, S, _ = x.shape
    proj = x @ w_in
    qw = C_HEADS * C_HEAD_DIM
    kw = C_KV_HEADS * C_HEAD_DIM
    q = proj[..., :qw].reshape(B, S, C_HEADS, C_HEAD_DIM)
    k = proj[..., qw:qw + kw].reshape(B, S, C_KV_HEADS, C_HEAD_DIM)
    v = proj[..., qw + kw:].reshape(B, S, C_KV_HEADS, C_HEAD_DIM)
    o = window_attention(q, k, v, sinks, bias)
    return o.astype(x.dtype) @ w_out


def swiglu(x, w1, w3, w2):
    return (jax.nn.silu(x @ w1) * (x @ w3)) @ w2


def _normal(key, shape, scale):
    return jax.random.normal(key, shape, jnp.float32) * scale


def setup_inputs(seed: int = 0) -> dict:
    key = jax.random.key(seed)
    ks = jax.random.split(key, 18)
    return {
        'x_prompt': _normal(ks[0], (BATCH, SEQ, D_MODEL), 1.0),
        'x_sample': _normal(ks[1], (DEC_BATCH, DEC_SEQ, D_MODEL), 1.0),
        'rel_bias_table': _normal(ks[2], (N_BUCKETS, C_HEADS), 0.5),
        'ab_w_in': _normal(ks[3], (N_EVEN, D_MODEL, AB_IN), D_MODEL ** -0.5),
        'ab_fourier_g': 1.0 + _normal(ks[4], (N_EVEN, A_GROUPS, A_GROUP_DIM), 0.05),
        'ab_gate_w2': _normal(ks[5], (N_EVEN, 2, GATE_RANK, B_QK_WIDTH), GATE_RANK ** -0.5),
        'ab_gate_b': _normal(ks[6], (N_EVEN, 2, B_QK_WIDTH), 0.1),
        'ab_head_norm_g': 1.0 + _normal(ks[7], (N_EVEN, B_HEADS, B_DV), 0.05),
        'ab_w_out': _normal(ks[8], (N_EVEN, AB_MIX, D_MODEL), DEEPNORM_BETA * AB_MIX ** -0.5),
        'c_w_in': _normal(ks[9], (N_ODD, D_MODEL, C_IN), D_MODEL ** -0.5),
        'c_sinks': _normal(ks[10], (N_ODD, C_HEADS), 0.5),
        'c_w_out': _normal(ks[11], (N_ODD, C_HEADS * C_HEAD_DIM, D_MODEL), DEEPNORM_BETA * (C_HEADS * C_HEAD_DIM) ** -0.5),
        'ffn_w1': _normal(ks[12], (DEPTH, D_MODEL, D_FF), D_MODEL ** -0.5),
        'ffn_w3': _normal(ks[13], (DEPTH, D_MODEL, D_FF), D_MODEL ** -0.5),
        'ffn_w2': _normal(ks[14], (DEPTH, D_FF, D_MODEL), DEEPNORM_BETA * D_FF ** -0.5),
        'ln_g': 1.0 + _normal(ks[15], (DEPTH, 2, D_MODEL), 0.05),
        'ln_b': _normal(ks[16], (DEPTH, 2, D_MODEL), 0.02),
    }


def reference(x_prompt, x_sample, rel_bias_table, ab_w_in, ab_fourier_g, ab_gate_w2, ab_gate_b,
              ab_head_norm_g, ab_w_out, c_w_in, c_sinks, c_w_out, ffn_w1, ffn_w3, ffn_w2, ln_g, ln_b):
    bias = band_bias(rel_bias_table)

    def trunk(x):
        for i in range(DEPTH):
            j = i // 2
            if i % 2 == 0:
                h = mixer_ab(x, ab_w_in[j], ab_fourier_g[j], ab_gate_w2[j], ab_gate_b[j],
                             ab_head_norm_g[j], ab_w_out[j])
            else:
                h = mixer_c(x, c_w_in[j], c_sinks[j], c_w_out[j], bias)
            x = layer_norm(DEEPNORM_ALPHA * x + h, ln_g[i, 0], ln_b[i, 0])
            f = swiglu(x, ffn_w1[i], ffn_w3[i], ffn_w2[i])
            x = layer_norm(DEEPNORM_ALPHA * x + f, ln_g[i, 1], ln_b[i, 1])
        return x

    y_prompt = trunk(x_prompt)
    y_sample = trunk(x_sample)
    return (y_prompt, y_sample)
```

```python
import math
import os
from contextlib import ExitStack
import numpy as np
import ml_dtypes
import concourse.bass as bass
import concourse.mybir as mybir
from concourse.bass_utils import run_bass_kernel_spmd

F32 = mybir.dt.float32
BF16 = mybir.dt.bfloat16
AF = mybir.ActivationFunctionType
ALU = mybir.AluOpType

PE, ACT, DVE, POOL, SP = "pe", "act", "dve", "pool", "sp"
ENGS = [PE, ACT, DVE, POOL, SP]
N_DMA_SEMS = 48

D = 4096
DFF = 11008
NFF = 86
AB_IN = 8224
ALPHA = 4 ** 0.25
EPS = 1e-5
NEG = -30000.0


class Buf:
    __slots__ = ("w", "wl", "r", "rd", "x")

    def __init__(self, x=False):
        self.w = None
        self.wl = []
        self.r = {}
        self.rd = []
        self.x = x


def bufs(n):
    return [Buf() for _ in range(n)]


class Op:
    __slots__ = ("eng", "fn", "deps", "needed", "dma", "sem", "val", "ticket")

    def __init__(self, eng, fn, dma):
        self.eng = eng
        self.fn = fn
        self.deps = []
        self.needed = False
        self.dma = dma
        self.sem = None
        self.val = None
        self.ticket = 0


class Prog:
    def __init__(self):
        self.ops = {e: [] for e in ENGS}
        self.slot_last = [None] * N_DMA_SEMS
        self.slot_cnt = [0] * N_DMA_SEMS
        self.rr = 0
        self.stores = []
        self.pending = {}

    def barrier(self):
        lasts = []
        for e in ENGS:
            for op in reversed(self.ops[e]):
                if not op.dma:
                    lasts.append(op)
                    break
        lasts += [o for o in self.slot_last if o is not None]
        for e in ENGS:
            self.pending[e] = list(lasts)

    def add(self, eng, fn, reads=(), writes=(), dma=False, is_store=False):
        op = Op(eng, fn, dma)
        deps = {}

        def dep(d, kind):
            if d is None:
                return
            if (not d.dma) and (not dma) and d.eng == eng:
                if kind != "raw" or eng == PE:
                    return
            deps[id(d)] = d

        for b in reads:
            dep(b.w, "raw")
            for d_ in b.wl:
                dep(d_, "raw")
            if b.x:
                for r in b.r.values():
                    if r.eng != eng:
                        deps[id(r)] = r
        for b in writes:
            dep(b.w, "waw")
            for d_ in b.wl:
                dep(d_, "waw")
            for r in b.r.values():
                dep(r, "war")
            for r in b.rd:
                dep(r, "war")
        pend = self.pending.pop(eng, None)
        if pend:
            for d_ in pend:
                if d_.dma or d_.eng != eng:
                    deps[id(d_)] = d_
        if dma:
            s = self.rr
            self.rr = (self.rr + 1) % N_DMA_SEMS
            prev = self.slot_last[s]
            if prev is not None:
                deps[id(prev)] = prev
            self.slot_cnt[s] += 1
            op.sem = s
            op.val = 16 * self.slot_cnt[s]
            self.slot_last[s] = op
            if is_store:
                self.stores.append(op)
        op.deps = list(deps.values())
        for d_ in op.deps:
            d_.needed = True
        for b in reads:
            if dma:
                b.rd.append(op)
                if len(b.rd) > N_DMA_SEMS:
                    b.rd = b.rd[-N_DMA_SEMS:]
            else:
                b.r[eng] = op
        for b in writes:
            if dma:
                b.w = None
                b.wl = [op]
            else:
                b.w = op
                b.wl = []
            b.r = {}
            b.rd = []
        self.ops[eng].append(op)
        return op

    def emit(self, nc):
        with ExitStack() as st:
            esem = {e: st.enter_context(nc.semaphore("s_" + e)) for e in ENGS}
            dsem = [st.enter_context(nc.semaphore("d%d" % i)) for i in range(N_DMA_SEMS)]
            for e in ENGS:
                c = 0
                for op in self.ops[e]:
                    if not op.dma and op.needed:
                        c += 1
                        op.ticket = c
                if os.environ.get('K_VERBOSE'):
                    print("ENG", e, "ops", len(self.ops[e]), "tickets", c, flush=True)
            if os.environ.get('K_VERBOSE'):
                print("DMA slot max", max(self.slot_cnt) * 16, flush=True)
            final_waits = {}
            for op in self.stores:
                final_waits[op.sem] = max(final_waits.get(op.sem, 0), op.val)
            for op in self.slot_last:
                if op is not None:
                    final_waits[op.sem] = max(final_waits.get(op.sem, 0), op.val)
            block = st.enter_context(nc.Block())

            def run(e):
                def body(eng):
                    known = {}
                    for op in self.ops[e]:
                        w = {}
                        for d_ in op.deps:
                            if d_.dma:
                                k = ("d", d_.sem)
                                v = d_.val
                            else:
                                k = ("e", d_.eng)
                                v = d_.ticket
                            if v > w.get(k, 0):
                                w[k] = v
                        for k, v in w.items():
                            if known.get(k, 0) >= v:
                                continue
                            known[k] = v
                            eng.wait_ge(dsem[k[1]] if k[0] == "d" else esem[k[1]], v)
                        ins = op.fn(eng)
                        if op.dma:
                            ins.then_inc(dsem[op.sem], 16)
                        elif op.needed:
                            ins.then_inc(esem[e], 1)
                    if e == SP:
                        for s, v in final_waits.items():
                            if known.get(("d", s), 0) < v:
                                eng.wait_ge(dsem[s], v)
                return body

            block.tensor(run(PE))
            block.scalar(run(ACT))
            block.vector(run(DVE))
            block.gpsimd(run(POOL))
            block.sync(run(SP))


class Arena:
    def __init__(self, t, width):
        self.t = t
        self.width = width
        self.off = 0

    def reset(self, off=0):
        self.off = off

    def _take(self, words):
        a = self.off
        self.off += words
        assert self.off <= self.width, ("arena overflow", self.off, self.width)
        return a

    def f32(self, *shape):
        n = int(np.prod(shape))
        a = self._take(n)
        ap = self.t[:, a:a + n]
        return self._shape(ap, shape)

    def bf16(self, *shape):
        n = int(np.prod(shape))
        w = (n + 1) // 2
        a = self._take(w)
        ap = self.t[:, a:a + w].bitcast(BF16)
        if n != 2 * w:
            ap = ap[:, 0:n]
        return self._shape(ap, shape)

    @staticmethod
    def _shape(ap, shape):
        if len(shape) == 1:
            return ap
        if len(shape) == 2:
            return ap.rearrange("p (a b) -> p a b", a=shape[0], b=shape[1])
        if len(shape) == 3:
            return ap.rearrange("p (a b c) -> p a b c", a=shape[0], b=shape[1], c=shape[2])
        raise ValueError(shape)


class _Stop(Exception):
    pass


_HOLD = {}


def build(NT, stop=None, DFF=11008):
    try:
        return _build(NT, stop, DFF)
    except _Stop:
        return _HOLD["nc"]


def _build(NT, stop=None, DFF=11008):
    NFF = DFF // 128
    nc = bass.Bass("TRN2", target_bir_lowering=False)
    _HOLD["nc"] = nc

    def ck(k):
        if stop is not None and stop == k:
            p.emit(nc)
            raise _Stop()

    NCH = NT // 128
    NTT = NT // 512
    p = Prog()

    mini = stop is not None and stop <= 3
    BIGW = ("ab_w_out", "c_w_in", "c_w_out", "ffn_w1", "ffn_w3", "ffn_w2")

    def din(name, shape, dt=F32):
        if mini and name in BIGW:
            shape = [128, 128]
        return nc.dram_tensor(name, list(shape), dt, kind="ExternalInput").ap()

    def dscr(name, shape, dt):
        big = name.startswith("w") and name != "wabin_s"
        if mini and big:
            shape = [128, 128]
        kind = "ExternalOutput" if (stop is not None and not big and name != "wabin_s") else "Internal"
        return nc.dram_tensor(name, list(shape), dt, kind=kind).ap()

    x_d = din("x", [NT, D])
    wabin_d = din("ab_w_in", [D, AB_IN])
    wabout_d = din("ab_w_out", [D, D])
    wcin_d = din("c_w_in", [D, 5120])
    wcout_d = din("c_w_out", [D, D])
    if mini:
        w1_d = w3_d = w2_d = None
        din("ffn_w1", [1]); din("ffn_w3", [1]); din("ffn_w2", [1])
    else:
        w1_d = din("ffn_w1", [2, D, DFF])
        w3_d = din("ffn_w3", [2, D, DFF])
        w2_d = din("ffn_w2", [2, DFF, D])
    fg_d = din("fourier_g", [4, 512])
    hg_d = din("head_g", [4, 512])
    gw2_d = din("gate_w2", [16, 2, 1024])
    gb_d = din("gate_b", [1, 2, 1024])
    lng_d = din("lng", [128, 4, 32])
    lnb_d = din("lnb", [128, 4, 32])
    sinks_d = din("sinks", [1, 64])
    bm_d = din("bm", [128, 3, 64 * 128])
    maskadd_d = din("maskadd", [128, 3, 128])
    vflag_d = din("vflag", [128, NCH * 3])
    keep_d = din("keep", [128, 2 * (NCH + 1)])
    dftc_d = din("dftc", [NT, NT], BF16)
    dfts_d = din("dfts", [NT, NT], BF16)
    fch_d = din("fch", [128, 4, 1024], BF16)
    ident_d = din("ident", [128, 128])
    identb_d = din("identb", [128, 128], BF16)
    glam_d = din("glam", [128, 4, 128])
    smask_d = din("smask", [128, 2, 128])
    y_d = nc.dram_tensor("y", [NT, D], F32, kind="ExternalOutput").ap()

    wabin_s = dscr("wabin_s", [D, AB_IN], BF16)
    wabout_s = dscr("wabout_s", [D, D], BF16)
    wcin_s = dscr("wcin_s", [D, 5632], BF16)
    wcout_s = dscr("wcout_s", [D, D], BF16)
    w1_s = [dscr("w1_s%d" % l, [D, DFF], BF16) for l in range(2)]
    w3_s = [dscr("w3_s%d" % l, [D, DFF], BF16) for l in range(2)]
    w2_s = [dscr("w2_s%d" % l, [DFF, D], BF16) for l in range(2)]
    zres_s = dscr("zres_s", [D, NT], F32)
    un_s = dscr("un_s", [NT, 2048], BF16)
    qk_s = dscr("qk_s", [NT, 2048], F32)
    vv_s = dscr("vv_s", [NT, 2048], BF16)
    rr_s = dscr("rr_s", [NT, 2048], F32)
    sp_s = dscr("sp_s", [NT, 2048], F32)
    ob_s = dscr("ob_s", [NT, 2048], F32)
    mixT_s = dscr("mixT_s", [D, NT], BF16)
    qT_s = dscr("qT_s", [5120, NT], BF16)
    v2_s = dscr("v2_s", [NT, 1024], BF16)
    oT_s = dscr("oT_s", [D, NT], BF16)

    b_wabin = bufs(32); b_wabout = bufs(32); b_wcin = bufs(32); b_wcout = bufs(32)
    b_w1 = [bufs(32), bufs(32)]; b_w3 = [bufs(32), bufs(32)]; b_w2 = [bufs(NFF), bufs(NFF)]

    SKIP = os.environ.get('K_SKIP', '')

    def cast_rows(dst, src, bl, nblk):
        for rb in range(0 if 'c' in SKIP else nblk):
            p.add(POOL, lambda e, rb=rb: e.dma_start(out=dst[rb * 128:(rb + 1) * 128, :], in_=src[rb * 128:(rb + 1) * 128, :]),
                  writes=[bl[rb]], dma=True)

    cast_rows(wabin_s, wabin_d, b_wabin, 32)
    if not mini:
      cast_rows(wabout_s, wabout_d, b_wabout, 32)
    for l in range(0 if mini else 1):
        cast_rows(w1_s[l], w1_d[l], b_w1[l], 32)
        cast_rows(w3_s[l], w3_d[l], b_w3[l], 32)
        cast_rows(w2_s[l], w2_d[l], b_w2[l], NFF)
    for rb in range(0 if mini else 32):
        r0 = rb * 128
        p.add(POOL, lambda e, r0=r0: e.dma_start(out=wcin_s[r0:r0 + 128, 0:4096], in_=wcin_d[r0:r0 + 128, 0:4096]),
              writes=[b_wcin[rb]], dma=True)
    b_wcin2 = bufs(32); b_wcin3 = bufs(32); b_wcin4 = bufs(32)
    for rb in range(0 if mini else 32):
        r0 = rb * 128
        p.add(POOL, lambda e, r0=r0: e.dma_start(out=wcin_s[r0:r0 + 128, 5120:5632], in_=wcin_d[r0:r0 + 128, 4608:5120]),
              writes=[b_wcin2[rb]], dma=True)
        for u, bl in ((0, b_wcin3), (1, b_wcin4)):
            dst = wcin_s[r0:r0 + 128, 4096:5120].rearrange("r (g u d) -> r g u d", g=8, u=2, d=64)[:, :, u, :]
            src = wcin_d[r0:r0 + 128, 4096:4608].rearrange("r (g d) -> r g d", g=8, d=64)
            p.add(POOL, lambda e, dst=dst, src=src: e.dma_start(out=dst, in_=src), writes=[bl[rb]], dma=True)
    if not mini:
      cast_rows(wcout_s, wcout_d, b_wcout, 32)
    for l in range(1, 1 if mini else 2):
        cast_rows(w1_s[l], w1_d[l], b_w1[l], 32)
        cast_rows(w3_s[l], w3_d[l], b_w3[l], 32)
        cast_rows(w2_s[l], w2_d[l], b_w2[l], NFF)
    b_wcin_all = b_wcin + b_wcin2 + b_wcin3 + b_wcin4

    AW = 51000
    arena_t = nc.alloc_sbuf_tensor("arena", [128, AW], F32)
    ar = Arena(arena_t, AW)
    psum = [nc.alloc_psum_tensor("ps%d" % i, [128, 512], F32) for i in range(8)]
    b_ps = [Buf(True) for _ in range(8)]
    pcount = [0]

    def bank():
        i = pcount[0] % 8
        pcount[0] += 1
        return psum[i], b_ps[i]

    def mm(out_ap, pairs, reads, writes):
        def fn(e):
            n = len(pairs)
            ins = None
            for i, (l, r) in enumerate(pairs):
                ins = e.matmul(out_ap, l, r, start=(i == 0), stop=(i == n - 1))
            return ins
        return p.add(PE, fn, reads=reads, writes=writes)

    def load(out, in_, reads, writes, eng=SP):
        return p.add(eng, lambda e: e.dma_start(out=out, in_=in_), reads=reads, writes=writes, dma=True)

    def store(out, in_, reads, writes=(), eng=SP, final=False):
        return p.add(eng, lambda e: e.dma_start(out=out, in_=in_), reads=reads, writes=writes, dma=True, is_store=final)

    ident = ar.f32(128)
    identb = ar.bf16(128)
    ones_bf = ar.bf16(128)
    lng = ar.f32(4, 32); lnb = ar.f32(4, 32); lnga = ar.f32(4, 32); lnba = ar.f32(4, 32)
    b_const = Buf()
    load(ident, ident_d, [], [b_const])
    load(identb, identb_d, [], [b_const])
    if 'l' not in SKIP:
        load(lng, lng_d, [], [b_const])
        load(lnb, lnb_d, [], [b_const])
    p.add(DVE, lambda e: e.memset(ones_bf, 1.0), writes=[b_const])
    p.add(DVE, lambda e: e.tensor_scalar(out=lnga, in0=lng, scalar1=ALPHA, scalar2=None, op0=ALU.mult), reads=[b_const], writes=[b_const])
    p.add(DVE, lambda e: e.tensor_scalar(out=lnba, in0=lnb, scalar1=ALPHA, scalar2=None, op0=ALU.mult), reads=[b_const], writes=[b_const])
    BASE = ar.off
    b_scr = {k: Buf() for k in ["zres", "un", "qk", "vv", "rr", "sp", "ob", "mixT", "qT", "v2", "oT"]}

    ar.reset(BASE)
    xT = ar.bf16(32, 512); b_xT = Buf()
    xst = [ar.f32(4096) for _ in range(2)]; b_xst = bufs(2)
    Wt = [ar.bf16(32, 512) for _ in range(2)]; b_Wt = bufs(2)
    Wg = ar.bf16(32, 32); b_Wg = Buf()
    zst = [ar.f32(512) for _ in range(2)]; b_zst = bufs(2)
    fgt = ar.f32(4, 512); gw2 = ar.f32(2, 1024); gbt = ar.f32(2, 1024); ones_f = ar.f32(128)
    b_c1 = Buf()
    for g in range(0 if 'f' in SKIP else 4):
        load(fgt[:, g, :], fg_d[g:g + 1, :].partition_broadcast(128), [], [b_c1])
    if 'g' not in SKIP:
        load(gw2[0:16], gw2_d, [], [b_c1])
        load(gbt[0:1], gb_d, [], [b_c1])
    p.add(DVE, lambda e: e.memset(ones_f, 1.0), writes=[b_c1])
    ost32 = [ar.f32(512) for _ in range(3)]; b_o32 = bufs(3)
    ostbf = [ar.bf16(512) for _ in range(3)]; b_obf = bufs(3)
    tmp32 = [ar.f32(512) for _ in range(2)]; b_t32 = bufs(2)
    stat = [ar.f32(8) for _ in range(4)]; b_stat = bufs(4)
    lrT = ar.f32(2, 512); b_lrT = Buf()
    cnt = {"w": 0, "o32": 0, "obf": 0, "t32": 0, "st": 0, "zst": 0}

    def rot(key, n):
        i = cnt[key] % n
        cnt[key] += 1
        return i

    def gn_stats(src_ap, b_src, width, st, b_st, junk, b_junk):
        p.add(ACT, lambda e: e.activation(out=junk, in_=src_ap, func=AF.Copy, accum_out=st[:, 0:1]), reads=[b_src], writes=[b_junk, b_st])
        p.add(ACT, lambda e: e.activation(out=junk, in_=src_ap, func=AF.Square, accum_out=st[:, 1:2]), reads=[b_src], writes=[b_junk, b_st])
        p.add(DVE, lambda e: e.tensor_scalar(out=st[:, 2:3], in0=st[:, 0:1], scalar1=1.0 / width, scalar2=None, op0=ALU.mult), reads=[b_st], writes=[b_st])
        p.add(DVE, lambda e: e.tensor_tensor(out=st[:, 3:4], in0=st[:, 2:3], in1=st[:, 2:3], op=ALU.mult), reads=[b_st], writes=[b_st])
        p.add(DVE, lambda e: e.scalar_tensor_tensor(out=st[:, 4:5], in0=st[:, 1:2], scalar=1.0 / width, in1=st[:, 3:4], op0=ALU.mult, op1=ALU.subtract), reads=[b_st], writes=[b_st])
        p.add(DVE, lambda e: e.tensor_scalar(out=st[:, 6:7], in0=st[:, 4:5], scalar1=EPS, scalar2=None, op0=ALU.add), reads=[b_st], writes=[b_st])
        p.add(ACT, lambda e: e.activation(out=st[:, 7:8], in_=st[:, 6:7], func=AF.Sqrt), reads=[b_st], writes=[b_st])
        p.add(DVE, lambda e: e.reciprocal(out=st[:, 5:6], in_=st[:, 7:8]), reads=[b_st], writes=[b_st])

    CUT = int(os.environ.get('K_CUT', '9'))
    b_ser = Buf()
    for tt in range(NTT if CUT >= 2 else 0):
        t0 = tt * 512
        for ts in range(4):
            xi = rot("w", 2)
            load(xst[xi], x_d[t0 + ts * 128:t0 + (ts + 1) * 128, :], [], [b_xst[xi]])
            for kq in range(8):
                ps, bp = bank()
                def fn(e, ps=ps, xi=xi, kq=kq):
                    ins = None
                    for k4 in range(4):
                        kc = kq * 4 + k4
                        ins = e.transpose(ps[:, k4 * 128:(k4 + 1) * 128], xst[xi][:, kc * 128:(kc + 1) * 128], ident)
                    return ins
                p.add(PE, fn, reads=[b_xst[xi], b_const], writes=[bp])
                p.add(ACT, lambda e, ps=ps, kq=kq, ts=ts: e.activation(
                    out=xT[:, kq * 4:(kq + 1) * 4, ts * 128:(ts + 1) * 128],
                    in_=ps[:].rearrange("p (a b) -> p a b", a=4, b=128), func=AF.Copy), reads=[bp], writes=[b_xT, b_ser])
                zi = rot("zst", 2)
                p.add(DVE, lambda e, ps=ps, zi=zi: e.tensor_scalar(out=zst[zi], in0=ps[:], scalar1=ALPHA, scalar2=None, op0=ALU.mult),
                      reads=[bp, b_ser], writes=[b_zst[zi]])
                dst = zres_s[kq * 512:(kq + 1) * 512, t0 + ts * 128:t0 + (ts + 1) * 128].rearrange("(a p) t -> p a t", p=128)
                store(dst, zst[zi].rearrange("p (a b) -> p a b", a=4, b=128), [b_zst[zi]], [b_scr["zres"]])
        if CUT < 3:
            continue
        load(Wg, wabin_s[:, 8192:8224].rearrange("(kc p) n -> p kc n", p=128), b_wabin, [b_Wg])
        for d_ in range(2):
            ps, bp = bank()
            mm(ps[0:16, :], [(Wg[:, kc, d_ * 16:(d_ + 1) * 16], xT[:, kc, :]) for kc in range(32)], [b_Wg, b_xT], [bp])
            p.add(ACT, lambda e, ps=ps, d_=d_: e.activation(out=lrT[0:16, d_, :], in_=ps[0:16, :], func=AF.Copy), reads=[bp], writes=[b_lrT])
        for ts in range(4):
            for d_ in range(2):
                for hf in range(2):
                    ps, bp = bank()
                    mm(ps[:], [(lrT[0:16, d_, ts * 128:(ts + 1) * 128], gw2[0:16, d_, hf * 512:(hf + 1) * 512]),
                               (ones_f[0:1, :], gbt[0:1, d_, hf * 512:(hf + 1) * 512])], [b_lrT, b_c1], [bp])
                    oi = rot("o32", 3)
                    p.add(ACT, lambda e, ps=ps, oi=oi: e.activation(out=ost32[oi], in_=ps[:], func=AF.Exp, scale=-1.0), reads=[bp], writes=[b_o32[oi]])
                    p.add(ACT, lambda e, oi=oi: e.activation(out=ost32[oi], in_=ost32[oi], func=AF.Ln, bias=1.0), reads=[b_o32[oi]], writes=[b_o32[oi]])
                    store(sp_s[t0 + ts * 128:t0 + (ts + 1) * 128, d_ * 1024 + hf * 512:d_ * 1024 + (hf + 1) * 512], ost32[oi],
                          [b_o32[oi]], [b_scr["sp"]])
        if CUT < 4:
            continue
        for ct in range(16):
            wi = rot("w", 2)
            load(Wt[wi], wabin_s[:, ct * 512:(ct + 1) * 512].rearrange("(kc p) n -> p kc n", p=128), b_wabin, [b_Wt[wi]])
            for ts in range(4):
                ps, bp = bank()
                mm(ps[:], [(xT[:, kc, ts * 128:(ts + 1) * 128], Wt[wi][:, kc, :]) for kc in range(32)], [b_xT, b_Wt[wi]], [bp])
                rows = slice(t0 + ts * 128, t0 + (ts + 1) * 128)
                if ct < 4:
                    si = rot("st", 4); ti = rot("t32", 2); oi = rot("obf", 3)
                    gn_stats(ps[:], bp, 512.0, stat[si], b_stat[si], tmp32[ti], b_t32[ti])
                    p.add(DVE, lambda e, ps=ps, si=si, ti=ti: e.tensor_scalar(out=tmp32[ti], in0=ps[:], scalar1=stat[si][:, 2:3], scalar2=stat[si][:, 5:6],
                                                                         op0=ALU.subtract, op1=ALU.mult), reads=[bp, b_stat[si]], writes=[b_t32[ti]])
                    p.add(POOL, lambda e, ti=ti, oi=oi, ct=ct: e.tensor_tensor(out=ostbf[oi], in0=tmp32[ti], in1=fgt[:, ct, :], op=ALU.mult),
                          reads=[b_t32[ti], b_c1], writes=[b_obf[oi]])
                    store(un_s[rows, ct * 512:(ct + 1) * 512], ostbf[oi], [b_obf[oi]], [b_scr["un"]])
                elif ct < 8:
                    oi = rot("o32", 3)
                    p.add(ACT, lambda e, ps=ps, oi=oi: e.activation(out=ost32[oi], in_=ps[:], func=AF.Copy), reads=[bp], writes=[b_o32[oi]])
                    store(qk_s[rows, (ct - 4) * 512:(ct - 3) * 512], ost32[oi], [b_o32[oi]], [b_scr["qk"]])
                elif ct < 12:
                    oi = rot("obf", 3)
                    p.add(ACT, lambda e, ps=ps, oi=oi: e.activation(out=ostbf[oi], in_=ps[:], func=AF.Copy), reads=[bp], writes=[b_obf[oi]])
                    store(vv_s[rows, (ct - 8) * 512:(ct - 7) * 512], ostbf[oi], [b_obf[oi]], [b_scr["vv"]])
                else:
                    oi = rot("o32", 3)
                    p.add(DVE, lambda e, ps=ps, oi=oi: e.tensor_copy(out=ost32[oi], in_=ps[:]), reads=[bp], writes=[b_o32[oi]])
                    store(rr_s[rows, (ct - 12) * 512:(ct - 11) * 512], ost32[oi], [b_o32[oi]], [b_scr["rr"]])
    p.barrier()
    ck(1)

    ar.reset(BASE)
    NSC = NCH
    Cs = [ar.bf16(NSC, 512) for _ in range(2)]; Ss = [ar.bf16(NSC, 512) for _ in range(2)]
    b_Cs = bufs(2); b_Ss = bufs(2)
    ung = ar.bf16(NSC, 512); b_ung = Buf()
    fch = ar.bf16(4, 1024); b_fch = Buf()
    QT = [ar.bf16(8, 512) for _ in range(2)]; b_QT = bufs(2)
    fo = [ar.bf16(512) for _ in range(3)]; b_fo = bufs(3)
    load(fch, fch_d, [], [b_fch])
    cq = 0; cf = 0; cs_i = 0
    for g in range(4):
        load(ung, un_s[:, g * 512:(g + 1) * 512].rearrange("(sc p) c -> p sc c", p=128), [b_scr["un"]], [b_ung])
        for st_ in range(NTT):
            ci = cs_i % 2; cs_i += 1
            load(Cs[ci], dftc_d[:, st_ * 512:(st_ + 1) * 512].rearrange("(sc p) s -> p sc s", p=128), [], [b_Cs[ci]])
            load(Ss[ci], dfts_d[:, st_ * 512:(st_ + 1) * 512].rearrange("(sc p) s -> p sc s", p=128), [], [b_Ss[ci]])
            qi = cq % 2; cq += 1
            for part, (Tb, bT) in enumerate(((Cs[ci], b_Cs[ci]), (Ss[ci], b_Ss[ci]))):
                for cb in range(4):
                    ps, bp = bank()
                    mm(ps[:], [(ung[:, sc, cb * 128:(cb + 1) * 128], Tb[:, sc, :]) for sc in range(NSC)], [b_ung, bT], [bp])
                    eng = ACT if cb % 2 == 0 else DVE
                    if eng == ACT:
                        p.add(ACT, lambda e, ps=ps, qi=qi, part=part, cb=cb: e.activation(out=QT[qi][:, part * 4 + cb, :], in_=ps[:], func=AF.Copy),
                              reads=[bp], writes=[b_QT[qi]])
                    else:
                        p.add(DVE, lambda e, ps=ps, qi=qi, part=part, cb=cb: e.tensor_copy(out=QT[qi][:, part * 4 + cb, :], in_=ps[:]),
                              reads=[bp], writes=[b_QT[qi]])
            for cpb in range(4):
                ps, bp = bank()
                pairs = []
                for part in range(2):
                    for cb in range(4):
                        pairs.append((fch[:, cb, part * 512 + cpb * 128:part * 512 + (cpb + 1) * 128], QT[qi][:, part * 4 + cb, :]))
                mm(ps[:], pairs, [b_fch, b_QT[qi]], [bp])
                fi = cf % 3; cf += 1
                p.add(ACT, lambda e, ps=ps, fi=fi: e.activation(out=fo[fi], in_=ps[:], func=AF.Copy), reads=[bp], writes=[b_fo[fi]])
                r0 = g * 512 + cpb * 128
                store(mixT_s[r0:r0 + 128, st_ * 512:(st_ + 1) * 512], fo[fi], [b_fo[fi]], [b_scr["mixT"]])
    p.barrier()
    ck(2)

    ar.reset(BASE)
    glam = ar.f32(4, 128); smask = ar.f32(2, 128); keep = ar.f32(2 * (NCH + 1)); hgt = ar.f32(4, 512)
    negs = ar.f32(1)
    b_c2 = Buf()
    load(glam, glam_d, [], [b_c2]); load(smask, smask_d, [], [b_c2]); load(keep, keep_d, [], [b_c2])
    for h in range(4):
        load(hgt[:, h, :], hg_d[h:h + 1, :].partition_broadcast(128), [], [b_c2])
    p.add(DVE, lambda e: e.memset(negs, -1.0 / 16), writes=[b_c2])
    S32 = ar.f32(8, 512); Sbf = ar.bf16(8, 512); b_S32 = bufs(8); b_Sbf = bufs(8)
    inq = [ar.f32(1024) for _ in range(2)]; ink = [ar.f32(1024) for _ in range(2)]
    inv = [ar.bf16(2048) for _ in range(2)]; insp = [ar.f32(1024) for _ in range(2)]
    inr = [ar.f32(2048) for _ in range(2)]; inob = [ar.f32(2048) for _ in range(2)]
    b_in = [bufs(6) for _ in range(2)]
    e1 = ar.f32(1024); b_e1 = Buf()
    e2 = ar.f32(1024); b_e2 = Buf()
    qg = ar.bf16(1024); kg = ar.bf16(1024); kdec = ar.bf16(1024); b_qg = Buf(); b_kg = Buf(); b_kdec = Buf()
    qgT = ar.bf16(8, 128); kgT = ar.bf16(8, 128); b_qgT = Buf(); b_kgT = Buf()
    dec8 = ar.f32(8); b_dec = Buf()
    sTm = [ar.bf16(128) for _ in range(2)]; b_sTm = bufs(2)
    o32 = [ar.f32(512) for _ in range(2)]; b_o32g = bufs(2)
    t32 = [ar.f32(512) for _ in range(2)]; b_t32g = bufs(2)
    sil = [ar.f32(512) for _ in range(2)]; b_sil = bufs(2)
    bo = [ar.bf16(512) for _ in range(2)]; b_bo = bufs(2)
    boT = [ar.bf16(4, 128) for _ in range(2)]; b_boT = bufs(2)
    gst = [ar.f32(8) for _ in range(4)]; b_gst = bufs(4)
    gc = {"in": 0, "s": 0, "o": 0, "st": 0}

    for dirn in (1, 0):
        for i in range(8):
            p.add(POOL, lambda e, i=i: e.memset(S32[:, i, :], 0.0), writes=[b_S32[i]])
            p.add(POOL, lambda e, i=i: e.memset(Sbf[:, i, :], 0.0), writes=[b_Sbf[i]])
        order = list(range(NCH)) if dirn == 0 else list(range(NCH - 1, -1, -1))
        for ci_, c in enumerate(order):
            nxt = order[ci_ + 1] if ci_ + 1 < NCH else NCH
            rows = slice(c * 128, (c + 1) * 128)
            ii = gc["in"] % 2; gc["in"] += 1
            bi = b_in[ii]
            load(inq[ii], qk_s[rows, 0:1024], [b_scr["qk"]], [bi[0]])
            load(ink[ii], qk_s[rows, 1024:2048], [b_scr["qk"]], [bi[1]])
            load(inv[ii], vv_s[rows, :], [b_scr["vv"]], [bi[2]])
            load(insp[ii], sp_s[rows, dirn * 1024:(dirn + 1) * 1024], [b_scr["sp"]], [bi[3]])
            if dirn == 0:
                load(inr[ii], rr_s[rows, :], [b_scr["rr"]], [bi[4]])
                load(inob[ii], ob_s[rows, :], [b_scr["ob"]], [bi[5]])
            for hf in range(2):
                ps, bp = bank()
                mm(ps[:], [(glam[:, dirn * 2, :], insp[ii][:, hf * 512:(hf + 1) * 512])], [b_c2, bi[3]], [bp])
                sl = slice(hf * 512, (hf + 1) * 512)
                p.add(ACT, lambda e, ps=ps, sl=sl: e.activation(out=e1[:, sl], in_=ps[:], func=AF.Exp), reads=[bp], writes=[b_e1])
                p.add(ACT, lambda e, ps=ps, sl=sl: e.activation(out=e2[:, sl], in_=ps[:], func=AF.Exp, scale=-1.0), reads=[bp], writes=[b_e2])
            p.add(DVE, lambda e, ii=ii: e.scalar_tensor_tensor(out=qg, in0=inq[ii], scalar=1.0 / 16, in1=e1, op0=ALU.mult, op1=ALU.mult),
                  reads=[bi[0], b_e1], writes=[b_qg])
            p.add(POOL, lambda e, ii=ii: e.tensor_tensor(out=kg, in0=ink[ii], in1=e2, op=ALU.mult), reads=[bi[1], b_e2], writes=[b_kg])
            for hf in range(2):
                ps, bp = bank()
                mm(ps[:], [(glam[:, dirn * 2 + 1, :], insp[ii][:, hf * 512:(hf + 1) * 512])], [b_c2, bi[3]], [bp])
                sl = slice(hf * 512, (hf + 1) * 512)
                p.add(ACT, lambda e, ps=ps, sl=sl: e.activation(out=e1[:, sl], in_=ps[:], func=AF.Exp), reads=[bp], writes=[b_e1])
            p.add(DVE, lambda e, ii=ii: e.tensor_tensor(out=kdec, in0=ink[ii], in1=e1, op=ALU.mult), reads=[bi[1], b_e1], writes=[b_kdec])
            ps, bp = bank()
            def fn(e, ps=ps, ii=ii):
                ins = None
                for i in range(8):
                    ins = e.matmul(ps[:, i:i + 1], insp[ii][:, i * 128:(i + 1) * 128], negs, start=True, stop=True)
                return ins
            p.add(PE, fn, reads=[bi[3], b_c2], writes=[bp])
            p.add(ACT, lambda e, ps=ps: e.activation(out=dec8, in_=ps[:, 0:8], func=AF.Exp), reads=[bp], writes=[b_dec])
            kcol = dirn * (NCH + 1) + c
            p.add(DVE, lambda e, kcol=kcol: e.tensor_scalar(out=dec8, in0=dec8, scalar1=keep[:, kcol:kcol + 1], scalar2=None, op0=ALU.mult),
                  reads=[b_dec, b_c2], writes=[b_dec])
            for src, b_src, dstT, b_dstT in ((qg, b_qg, qgT, b_qgT), (kg, b_kg, kgT, b_kgT)):
                ps, bp = bank()
                psb = ps[:].bitcast(BF16)
                def fn(e, psb=psb, src=src):
                    ins = None
                    for i in range(8):
                        ins = e.transpose(psb[:, i * 128:(i + 1) * 128], src[:, i * 128:(i + 1) * 128], identb)
                    return ins
                p.add(PE, fn, reads=[b_src, b_const], writes=[bp])
                p.add(ACT, lambda e, psb=psb, dstT=dstT: e.activation(out=dstT, in_=psb.rearrange("p (a b) -> p a b", a=8, b=128), func=AF.Copy),
                      reads=[bp], writes=[b_dstT])
            for h in range(4):
                vh = inv[ii][:, h * 512:(h + 1) * 512]
                ps, bp = bank()
                mm(ps[:, 0:128], [(kgT[:, 2 * h + dc, :], qgT[:, 2 * h + dc, :]) for dc in range(2)], [b_kgT, b_qgT], [bp])
                si = gc["s"] % 2; gc["s"] += 1
                p.add(DVE, lambda e, ps=ps, si=si, dirn=dirn: e.tensor_tensor(out=sTm[si], in0=ps[:, 0:128], in1=smask[:, dirn, :], op=ALU.mult),
                      reads=[bp, b_c2], writes=[b_sTm[si]])
                ps, bp = bank()
                mm(ps[:], [(sTm[si], vh)] + [(qgT[:, 2 * h + dc, :], Sbf[:, 2 * h + dc, :]) for dc in range(2)],
                   [b_sTm[si], bi[2], b_qgT, b_Sbf[2 * h], b_Sbf[2 * h + 1]], [bp])
                oi = gc["o"] % 2; gc["o"] += 1
                hs = slice(h * 512, (h + 1) * 512)
                if dirn == 1:
                    p.add(ACT, lambda e, ps=ps, oi=oi: e.activation(out=o32[oi], in_=ps[:], func=AF.Copy), reads=[bp], writes=[b_o32g[oi]])
                    store(ob_s[rows, hs], o32[oi], [b_o32g[oi]], [b_scr["ob"]])
                else:
                    p.add(DVE, lambda e, ps=ps, oi=oi, ii=ii, hs=hs: e.tensor_tensor(out=o32[oi], in0=ps[:], in1=inob[ii][:, hs], op=ALU.add),
                          reads=[bp, bi[5]], writes=[b_o32g[oi]])
                    gi = gc["st"] % 4; gc["st"] += 1
                    gn_stats(o32[oi], b_o32g[oi], 512.0, gst[gi], b_gst[gi], t32[oi], b_t32g[oi])
                    p.add(DVE, lambda e, oi=oi, gi=gi: e.tensor_scalar(out=t32[oi], in0=o32[oi], scalar1=gst[gi][:, 2:3], scalar2=gst[gi][:, 5:6],
                                                                      op0=ALU.subtract, op1=ALU.mult), reads=[b_o32g[oi], b_gst[gi]], writes=[b_t32g[oi]])
                    p.add(POOL, lambda e, oi=oi, h=h: e.tensor_tensor(out=t32[oi], in0=t32[oi], in1=hgt[:, h, :], op=ALU.mult),
                          reads=[b_t32g[oi], b_c2], writes=[b_t32g[oi]])
                    p.add(ACT, lambda e, oi=oi, ii=ii, hs=hs: e.activation(out=sil[oi], in_=inr[ii][:, hs], func=AF.Silu), reads=[bi[4]], writes=[b_sil[oi]])
                    p.add(DVE, lambda e, oi=oi: e.tensor_tensor(out=bo[oi], in0=t32[oi], in1=sil[oi], op=ALU.mult),
                          reads=[b_t32g[oi], b_sil[oi]], writes=[b_bo[oi]])
                    ps2, bp2 = bank()
                    psb = ps2[:, 0:256].bitcast(BF16)
                    def fn(e, psb=psb, oi=oi):
                        ins = None
                        for i in range(4):
                            ins = e.transpose(psb[:, i * 128:(i + 1) * 128], bo[oi][:, i * 128:(i + 1) * 128], identb)
                        return ins
                    p.add(PE, fn, reads=[b_bo[oi], b_const], writes=[bp2])
                    p.add(ACT, lambda e, psb=psb, oi=oi: e.activation(out=boT[oi], in_=psb.rearrange("p (a b) -> p a b", a=4, b=128), func=AF.Copy),
                          reads=[bp2], writes=[b_boT[oi]])
                    r0 = 2048 + h * 512
                    store(mixT_s[r0:r0 + 512, rows].rearrange("(a p) t -> p a t", p=128), boT[oi], [b_boT[oi]], [b_scr["mixT"]])
                for dc in range(2):
                    i8 = 2 * h + dc
                    ps, bp = bank()
                    mm(ps[:], [(kdec[:, i8 * 128:(i8 + 1) * 128], vh)], [b_kdec, bi[2]], [bp])
                    p.add(DVE, lambda e, ps=ps, i8=i8: e.scalar_tensor_tensor(out=S32[:, i8, :], in0=S32[:, i8, :], scalar=dec8[:, i8:i8 + 1], in1=ps[:],
                                                                           op0=ALU.mult, op1=ALU.add), reads=[bp, b_dec, b_S32[i8]], writes=[b_S32[i8]])
                    kn = dirn * (NCH + 1) + nxt
                    p.add(ACT, lambda e, i8=i8, kn=kn: e.activation(out=Sbf[:, i8, :], in_=S32[:, i8, :], func=AF.Copy, scale=keep[:, kn:kn + 1]),
                          reads=[b_S32[i8], b_c2], writes=[b_Sbf[i8]])
    p.barrier()
    ck(3)

    def dense_chain(layer, srcT_s, b_src, win_s, b_win, final):
        ar.reset(BASE)
        xTc = ar.bf16(32, 512); b_x = bufs(32)
        z = ar.f32(32, 512); b_z = bufs(32)
        aT = ar.bf16(18, 512); b_aT = bufs(18)
        WR0 = ar.off
        Wa = [ar.bf16(32, 256) for _ in range(2)]
        ar.reset(WR0)
        Wb = [ar.bf16(18, 512) for _ in range(2)]
        ar.reset(WR0)
        Wv = ar.bf16(32, 512)
        R = bufs(4)
        b_Wa = [[R[0]], [R[1], R[2]]]
        b_Wb = [[R[0], R[1]], [R[2], R[3]]]
        b_Wv = [R[0], R[1], R[2]]
        ar.reset(WR0 + 9216)
        zb = [ar.bf16(512) for _ in range(2)]; zq = [ar.bf16(512) for _ in range(2)]; b_zb = bufs(2); b_zq = bufs(2)
        mean = ar.f32(512); rstd = ar.f32(512); msq = ar.f32(512); b_mr = Buf()
        tl = [ar.f32(512) for _ in range(2)]; b_tl = bufs(2)
        sl_ = [ar.f32(512) for _ in range(2)]; b_sl = bufs(2)
        qst = [ar.bf16(512) for _ in range(3)]; b_qst = bufs(3)
        vst = [ar.bf16(1024) for _ in range(2)]; b_vst = bufs(2)
        yst = [ar.f32(512) for _ in range(2)]; b_yst = bufs(2)
        cc = {"wa": 0, "wb": 0, "zb": 0, "tl": 0, "sl": 0, "q": 0, "v": 0, "y": 0}

        def layer_norm(lnidx, is_final, tok0):
            ps1, bp1 = bank()
            ps2, bp2 = bank()
            his = []
            for kc in range(32):
                i = cc["zb"] % 2; cc["zb"] += 1
                p.add(ACT, lambda e, kc=kc, i=i: e.activation(out=zb[i], in_=z[:, kc, :], func=AF.Copy), reads=[b_z[kc]], writes=[b_zb[i]])
                p.add(POOL, lambda e, kc=kc, i=i: e.tensor_tensor(out=zq[i], in0=z[:, kc, :], in1=z[:, kc, :], op=ALU.mult), reads=[b_z[kc]], writes=[b_zq[i]])
                p.add(PE, lambda e, kc=kc, i=i, ps1=ps1: e.matmul(ps1[:], ones_bf, zb[i], start=(kc == 0), stop=(kc == 31)),
                      reads=[b_zb[i], b_const], writes=[bp1])
                p.add(PE, lambda e, kc=kc, i=i, ps2=ps2: e.matmul(ps2[:], ones_bf, zq[i], start=(kc == 0), stop=(kc == 31)),
                      reads=[b_zq[i], b_const], writes=[bp2])
            p.add(DVE, lambda e: e.tensor_scalar(out=mean, in0=ps1[:], scalar1=1.0 / D, scalar2=None, op0=ALU.mult), reads=[bp1], writes=[b_mr])
            p.add(DVE, lambda e: e.tensor_tensor(out=msq, in0=mean, in1=mean, op=ALU.mult), reads=[b_mr], writes=[b_mr])
            p.add(DVE, lambda e: e.scalar_tensor_tensor(out=rstd, in0=ps2[:], scalar=1.0 / D, in1=msq, op0=ALU.mult, op1=ALU.subtract), reads=[bp2, b_mr], writes=[b_mr])
            p.add(DVE, lambda e: e.tensor_scalar(out=rstd, in0=rstd, scalar1=EPS, scalar2=None, op0=ALU.add), reads=[b_mr], writes=[b_mr])
            p.add(ACT, lambda e: e.activation(out=rstd, in_=rstd, func=AF.Sqrt), reads=[b_mr], writes=[b_mr])
            p.add(DVE, lambda e: e.reciprocal(out=rstd, in_=rstd), reads=[b_mr], writes=[b_mr])
            for kc in range(32):
                i = cc["tl"] % 2; cc["tl"] += 1
                p.add(DVE, lambda e, kc=kc, i=i: e.tensor_tensor(out=tl[i], in0=z[:, kc, :], in1=mean, op=ALU.subtract), reads=[b_z[kc], b_mr], writes=[b_tl[i]])
                p.add(POOL, lambda e, i=i: e.tensor_tensor(out=tl[i], in0=tl[i], in1=rstd, op=ALU.mult), reads=[b_tl[i], b_mr], writes=[b_tl[i]])
                if not is_final:
                    p.add(ACT, lambda e, kc=kc, i=i: e.activation(out=xTc[:, kc, :], in_=tl[i], func=AF.Identity,
                                                                 scale=lng[:, lnidx, kc:kc + 1], bias=lnb[:, lnidx, kc:kc + 1]),
                          reads=[b_tl[i], b_const], writes=[b_x[kc]])
                    p.add(ACT, lambda e, kc=kc, i=i: e.activation(out=z[:, kc, :], in_=tl[i], func=AF.Identity,
                                                                 scale=lnga[:, lnidx, kc:kc + 1], bias=lnba[:, lnidx, kc:kc + 1]),
                          reads=[b_tl[i], b_const], writes=[b_z[kc]])
                else:
                    p.add(ACT, lambda e, kc=kc, i=i: e.activation(out=z[:, kc, :], in_=tl[i], func=AF.Identity,
                                                                 scale=lng[:, lnidx, kc:kc + 1], bias=lnb[:, lnidx, kc:kc + 1]),
                          reads=[b_tl[i], b_const], writes=[b_z[kc]])
                    ps, bp = bank()
                    def fn(e, ps=ps, kc=kc):
                        ins = None
                        for ts in range(4):
                            ins = e.transpose(ps[:, ts * 128:(ts + 1) * 128], z[:, kc, ts * 128:(ts + 1) * 128], ident)
                        return ins
                    p.add(PE, fn, reads=[b_z[kc], b_const], writes=[bp])
                    yi = cc["y"] % 2; cc["y"] += 1
                    p.add(DVE, lambda e, ps=ps, yi=yi: e.tensor_copy(out=yst[yi], in_=ps[:]), reads=[bp], writes=[b_yst[yi]])
                    dst = y_d[tok0:tok0 + 512, kc * 128:(kc + 1) * 128].rearrange("(a p) f -> p a f", p=128)
                    p.add(SP, lambda e, dst=dst, yi=yi: e.dma_start(out=dst, in_=yst[yi].rearrange("p (a b) -> p a b", a=4, b=128)),
                          reads=[b_yst[yi]], dma=True, is_store=True)

        def proj_res(w_s, b_w):
            for ip in range(16):
                wi = cc["wa"] % 2; cc["wa"] += 1
                load(Wa[wi], w_s[:, ip * 256:(ip + 1) * 256].rearrange("(kc p) n -> p kc n", p=128), b_w, b_Wa[wi])
                for i2 in range(2):
                    i = ip * 2 + i2
                    ps, bp = bank()
                    mm(ps[:], [(Wa[wi][:, kc, i2 * 128:(i2 + 1) * 128], xTc[:, kc, :]) for kc in range(32)], b_Wa[wi] + b_x, [bp])
                    p.add(DVE, lambda e, ps=ps, i=i: e.tensor_tensor(out=z[:, i, :], in0=z[:, i, :], in1=ps[:], op=ALU.add), reads=[bp, b_z[i]], writes=[b_z[i]])

        def ffn(l):
            ng = (NFF + 17) // 18
            groups = [((NFF * gi) // ng, (NFF * (gi + 1)) // ng) for gi in range(ng)]
            for (j0, j1) in groups:
                nj = j1 - j0
                jj = j0
                while jj < j1:
                    nb = min(2, j1 - jj)
                    w1i = cc["wa"] % 2; cc["wa"] += 1
                    load(Wa[w1i][:, :, 0:nb * 128], w1_s[l][:, jj * 128:(jj + nb) * 128].rearrange("(kc p) n -> p kc n", p=128), b_w1[l], b_Wa[w1i])
                    w3i = cc["wa"] % 2; cc["wa"] += 1
                    load(Wa[w3i][:, :, 0:nb * 128], w3_s[l][:, jj * 128:(jj + nb) * 128].rearrange("(kc p) n -> p kc n", p=128), b_w3[l], b_Wa[w3i])
                    for b2 in range(nb):
                        j = jj + b2
                        ps1, bp1 = bank()
                        mm(ps1[:], [(Wa[w1i][:, kc, b2 * 128:(b2 + 1) * 128], xTc[:, kc, :]) for kc in range(32)], b_Wa[w1i] + b_x, [bp1])
                        ps3, bp3 = bank()
                        mm(ps3[:], [(Wa[w3i][:, kc, b2 * 128:(b2 + 1) * 128], xTc[:, kc, :]) for kc in range(32)], b_Wa[w3i] + b_x, [bp3])
                        si = cc["sl"] % 2; cc["sl"] += 1
                        p.add(ACT, lambda e, ps1=ps1, si=si: e.activation(out=sl_[si], in_=ps1[:], func=AF.Silu), reads=[bp1], writes=[b_sl[si]])
                        p.add(DVE, lambda e, ps3=ps3, si=si, jl=j - j0: e.tensor_tensor(out=aT[:, jl, :], in0=sl_[si], in1=ps3[:], op=ALU.mult),
                              reads=[bp3, b_sl[si]], writes=[b_aT[j - j0]])
                    jj += nb
                for ib in range(8):
                    wi = cc["wb"] % 2; cc["wb"] += 1
                    load(Wb[wi][:, 0:nj, :], w2_s[l][j0 * 128:j1 * 128, ib * 512:(ib + 1) * 512].rearrange("(j p) n -> p j n", p=128),
                         b_w2[l][j0:j1], b_Wb[wi])
                    for i4 in range(4):
                        i = ib * 4 + i4
                        ps, bp = bank()
                        mm(ps[:], [(Wb[wi][:, jl, i4 * 128:(i4 + 1) * 128], aT[:, jl, :]) for jl in range(nj)], b_Wb[wi] + b_aT[0:nj], [bp])
                        p.add(DVE, lambda e, ps=ps, i=i: e.tensor_tensor(out=z[:, i, :], in0=z[:, i, :], in1=ps[:], op=ALU.add), reads=[bp, b_z[i]], writes=[b_z[i]])

        for tt in range(NTT):
            t0 = tt * 512
            for kc in range(32):
                pass
            p.add(SP, lambda e, t0=t0: e.dma_start(out=xTc, in_=srcT_s[:, t0:t0 + 512].rearrange("(kc p) t -> p kc t", p=128)),
                  reads=[b_src], writes=b_x, dma=True)
            p.add(SP, lambda e, t0=t0: e.dma_start(out=z, in_=zres_s[:, t0:t0 + 512].rearrange("(kc p) t -> p kc t", p=128)),
                  reads=[b_scr["zres"]], writes=b_z, dma=True)
            proj_res(win_s, b_win)
            layer_norm(layer * 2, False, t0)
            ffn(layer)
            layer_norm(layer * 2 + 1, final, t0)
            if not final:
                p.add(SP, lambda e, t0=t0: e.dma_start(out=zres_s[:, t0:t0 + 512].rearrange("(kc p) t -> p kc t", p=128), in_=z),
                      reads=b_z, writes=[b_scr["zres"]], dma=True)
                for ip in range(20):
                    wi = cc["wa"] % 2; cc["wa"] += 1
                    load(Wa[wi], wcin_s[:, ip * 256:(ip + 1) * 256].rearrange("(kc p) n -> p kc n", p=128), b_wcin_all, b_Wa[wi])
                    for i2 in range(2):
                        i = ip * 2 + i2
                        ps, bp = bank()
                        mm(ps[:], [(Wa[wi][:, kc, i2 * 128:(i2 + 1) * 128], xTc[:, kc, :]) for kc in range(32)], b_Wa[wi] + b_x, [bp])
                        qi = cc["q"] % 3; cc["q"] += 1
                        p.add(ACT, lambda e, ps=ps, qi=qi: e.activation(out=qst[qi], in_=ps[:], func=AF.Copy), reads=[bp], writes=[b_qst[qi]])
                        store(qT_s[i * 128:(i + 1) * 128, t0:t0 + 512], qst[qi], [b_qst[qi]], [b_scr["qT"]])
                load(Wv, wcin_s[:, 5120:5632].rearrange("(kc p) n -> p kc n", p=128), b_wcin_all, b_Wv)
                for ts in range(4):
                    ps, bp = bank()
                    mm(ps[:], [(xTc[:, kc, ts * 128:(ts + 1) * 128], Wv[:, kc, :]) for kc in range(32)], b_Wv + b_x, [bp])
                    vi = cc["v"] % 2; cc["v"] += 1
                    vv4 = vst[vi].rearrange("p (g u d) -> p g u d", g=8, u=2, d=64)
                    pv = ps[:].rearrange("p (g d) -> p g d", g=8, d=64)
                    p.add(ACT, lambda e, vv4=vv4, pv=pv: e.activation(out=vv4[:, :, 0, :], in_=pv, func=AF.Copy), reads=[bp], writes=[b_vst[vi]])
                    p.add(DVE, lambda e, vv4=vv4, pv=pv: e.tensor_copy(out=vv4[:, :, 1, :], in_=pv), reads=[bp], writes=[b_vst[vi]])
                    store(v2_s[t0 + ts * 128:t0 + (ts + 1) * 128, :], vst[vi], [b_vst[vi]], [b_scr["v2"]])
        p.barrier()

    dense_chain(0, mixT_s, b_scr["mixT"], wabout_s, b_wabout, False)
    ck(4)

    ar.reset(BASE)
    BM = ar.f32(3, 64 * 128); b_BM = Buf()
    madd = ar.f32(3, 128); vflag = ar.f32(NCH * 3); esr = ar.bf16(64 * 128); sk = ar.f32(64)
    b_c4 = Buf()
    for kb in range(3):
        load(BM[:, kb, :], bm_d[:, kb, :], [], [b_BM])
    load(madd, maskadd_d, [], [b_c4]); load(vflag, vflag_d, [], [b_c4]); load(sk[0:1], sinks_d, [], [b_c4])
    p.add(ACT, lambda e: e.activation(out=sk[0:1], in_=sk[0:1], func=AF.Exp), reads=[b_c4], writes=[b_c4])
    for h in range(64):
        p.add(DVE, lambda e, h=h: e.tensor_scalar(out=esr[0:1, h * 128:(h + 1) * 128], in0=ones_f[0:1, :], scalar1=sk[0:1, h:h + 1], scalar2=None, op0=ALU.mult),
              reads=[b_c4, b_c1], writes=[b_c4])
        for kb in range(3):
            p.add(POOL if h % 2 else DVE, lambda e, h=h, kb=kb: e.tensor_tensor(out=BM[:, kb, h * 128:(h + 1) * 128], in0=BM[:, kb, h * 128:(h + 1) * 128],
                                                                              in1=madd[:, kb, :], op=ALU.add), reads=[b_BM, b_c4], writes=[b_BM])
    QTb = [ar.bf16(32, 128) for _ in range(2)]; KTb = [ar.bf16(8, 384) for _ in range(2)]; Vb = [ar.bf16(3, 1024) for _ in range(2)]
    b_at = [bufs(7) for _ in range(2)]
    tS = [ar.f32(512) for _ in range(3)]; b_tS = bufs(3)
    PT = [ar.bf16(3, 512) for _ in range(2)]; b_PT = [bufs(3) for _ in range(2)]
    rec = [ar.f32(512) for _ in range(2)]; b_rec = bufs(2)
    oTt = [ar.bf16(4, 128) for _ in range(2)]; b_oTt = bufs(2)
    ac = {"in": 0, "t": 0, "p": 0, "r": 0, "o": 0}
    for n in range(NCH):
        ii = ac["in"] % 2; ac["in"] += 1
        cols = slice(n * 128, (n + 1) * 128)
        load(QTb[ii], qT_s[0:4096, cols].rearrange("(a p) t -> p a t", p=128), [b_scr["qT"]], [b_at[ii][0]])
        for kb in range(3):
            nb_ = min(max(n - 1 + kb, 0), NCH - 1)
            load(KTb[ii][:, :, kb * 128:(kb + 1) * 128], qT_s[4096:5120, nb_ * 128:(nb_ + 1) * 128].rearrange("(g p) t -> p g t", p=128),
                 [b_scr["qT"]], [b_at[ii][1 + kb]])
            load(Vb[ii][:, kb, :], v2_s[nb_ * 128:(nb_ + 1) * 128, :], [b_scr["v2"]], [b_at[ii][4 + kb]])
        for g in range(8):
            oi = ac["o"] % 2; ac["o"] += 1
            for a in range(2):
                pa = slice(64 * a, 64 * a + 64)
                pi = ac["p"] % 2; ac["p"] += 1
                hq = (g * 2 + a) * 4
                for kb in range(3):
                    ps, bp = bank()
                    mm(ps[:], [(KTb[ii][pa, g, kb * 128:(kb + 1) * 128], QTb[ii][pa, 4 * g:4 * g + 4, :])], [b_at[ii][0], b_at[ii][1 + kb]], [bp])
                    ti = ac["t"] % 3; ac["t"] += 1
                    p.add(DVE, lambda e, ps=ps, ti=ti, kb=kb, hq=hq: e.scalar_tensor_tensor(out=tS[ti], in0=ps[:], scalar=0.125, in1=BM[:, kb, hq * 128:(hq + 4) * 128],
                                                                                       op0=ALU.mult, op1=ALU.add), reads=[bp, b_BM], writes=[b_tS[ti]])
                    p.add(ACT, lambda e, ti=ti, pi=pi, kb=kb, n=n: e.activation(out=PT[pi][:, kb, :], in_=tS[ti], func=AF.Exp, bias=vflag[:, n * 3 + kb:n * 3 + kb + 1]),
                          reads=[b_tS[ti], b_c4], writes=[b_PT[pi][kb]])
                pso, bpo = bank()
                mm(pso[:], [(Vb[ii][:, kb, g * 128:(g + 1) * 128], PT[pi][:, kb, :]) for kb in range(3)], b_at[ii][4:7] + b_PT[pi], [bpo])
                psd, bpd = bank()
                mm(psd[:], [(ones_bf, PT[pi][:, kb, :]) for kb in range(3)] + [(ones_bf[0:1, :], esr[0:1, hq * 128:(hq + 4) * 128])],
                   [b_const, b_c4] + b_PT[pi], [bpd])
                ri = ac["r"] % 2; ac["r"] += 1
                p.add(DVE, lambda e, psd=psd, ri=ri, pa=pa: e.reciprocal(out=rec[ri][pa, :], in_=psd[pa, :]), reads=[bpd], writes=[b_rec[ri]])
                p.add(DVE, lambda e, pso=pso, ri=ri, pa=pa, oi=oi: e.tensor_tensor(out=oTt[oi][pa].rearrange("p a b -> p (a b)"), in0=pso[pa, :], in1=rec[ri][pa, :], op=ALU.mult),
                      reads=[bpo, b_rec[ri]], writes=[b_oTt[oi]])
            store(oT_s[g * 512:(g + 1) * 512, cols].rearrange("(a p) t -> p a t", p=128), oTt[oi], [b_oTt[oi]], [b_scr["oT"]])
    p.barrier()
    ck(5)

    dense_chain(1, oT_s, b_scr["oT"], wcout_s, b_wcout, True)
    p.emit(nc)
    return nc


def _t5_bucket(rel):
    nb = 16
    max_exact = 8
    ret = np.where(rel > 0, nb, 0)
    n = np.abs(rel)
    nf = np.maximum(n, 1).astype(np.float32)
    large = max_exact + (np.log(nf / max_exact) / math.log(128 / max_exact) * (nb - max_exact)).astype(np.int32)
    large = np.minimum(large, nb - 1)
    return ret + np.where(n < max_exact, n, large)


def _consts(NT, seq_lens):
    bf = ml_dtypes.bfloat16
    NCH = NT // 128
    c = {}
    dc = np.zeros((NT, NT), np.float32)
    ds = np.zeros((NT, NT), np.float32)
    o = 0
    for S in seq_lens:
        k = np.arange(S, dtype=np.int64)
        ang = 2 * np.pi * ((k[:, None] * k[None, :]) % S) / S
        dc[o:o + S, o:o + S] = np.cos(ang) / np.sqrt(S)
        ds[o:o + S, o:o + S] = np.sin(ang) / np.sqrt(S)
        o += S
    c["dftc"] = dc.astype(bf)
    c["dfts"] = ds.astype(bf)
    k = np.arange(512, dtype=np.int64)
    ang = 2 * np.pi * ((k[:, None] * k[None, :]) % 512) / 512
    F = np.concatenate([np.cos(ang), -np.sin(ang)], axis=1) / np.sqrt(512)
    c["fch"] = np.ascontiguousarray(F.reshape(4, 128, 1024).transpose(1, 0, 2)).astype(bf)
    c["ident"] = np.eye(128, dtype=np.float32)
    c["identb"] = np.eye(128).astype(bf)
    j = np.arange(128)[:, None]
    i = np.arange(128)[None, :]
    s = -1.0 / 16
    glam = np.stack([(j <= i) * s, (j > i) * s, (j >= i) * s, (j < i) * s], axis=1).astype(np.float32)
    c["glam"] = np.ascontiguousarray(glam)
    c["smask"] = np.ascontiguousarray(np.stack([(j <= i), (j >= i)], axis=1).astype(np.float32))
    starts = set()
    ends = set()
    o = 0
    for S in seq_lens:
        starts.add(o // 128)
        o += S
        ends.add(o // 128 - 1)
    keep = np.ones((2, NCH + 1), np.float32)
    for cch in range(NCH):
        if cch in starts:
            keep[0, cch] = 0.0
        if cch in ends:
            keep[1, cch] = 0.0
    c["keep"] = np.ascontiguousarray(np.broadcast_to(keep.reshape(1, -1), (128, 2 * (NCH + 1)))).astype(np.float32)
    vf = np.zeros((NCH, 3), np.float32)
    for n in range(NCH):
        if n in starts:
            vf[n, 0] = NEG
        if n in ends:
            vf[n, 2] = NEG
    c["vflag"] = np.ascontiguousarray(np.broadcast_to(vf.reshape(1, -1), (128, NCH * 3))).astype(np.float32)
    pk = np.arange(128)[:, None]
    q = np.arange(128)[None, :]
    madd = np.zeros((128, 3, 128), np.float32)
    for kb in range(3):
        rel = (kb - 1) * 128 + pk - q
        madd[:, kb, :] = np.where(np.abs(rel) <= 128, 0.0, NEG)
    c["maskadd"] = madd
    return c


def _head_order():
    return np.array([8 * g + 2 * jj + a for g in range(8) for a in range(2) for jj in range(4)], dtype=np.int64)


def _per_core_inputs(NT, xin, seq_lens, W):
    c = _consts(NT, seq_lens)
    m = dict(c)
    m["x"] = np.ascontiguousarray(xin, dtype=np.float32)
    m.update(W)
    return m


def _shared_weight_inputs(rel_bias_table, ab_w_in, ab_fourier_g, ab_gate_w2, ab_gate_b, ab_head_norm_g, ab_w_out,
                          c_w_in, c_sinks, c_w_out, ffn_w1, ffn_w3, ffn_w2, ln_g, ln_b):
    W = {}
    W["ab_w_in"] = np.ascontiguousarray(ab_w_in[0], dtype=np.float32)
    W["ab_w_out"] = np.ascontiguousarray(ab_w_out[0], dtype=np.float32)
    W["c_w_in"] = np.ascontiguousarray(c_w_in[0], dtype=np.float32)
    W["c_w_out"] = np.ascontiguousarray(c_w_out[0], dtype=np.float32)
    W["ffn_w1"] = np.ascontiguousarray(ffn_w1, dtype=np.float32)
    W["ffn_w3"] = np.ascontiguousarray(ffn_w3, dtype=np.float32)
    W["ffn_w2"] = np.ascontiguousarray(ffn_w2, dtype=np.float32)
    W["fourier_g"] = np.ascontiguousarray(ab_fourier_g[0], dtype=np.float32)
    W["head_g"] = np.ascontiguousarray(ab_head_norm_g[0], dtype=np.float32)
    W["gate_w2"] = np.ascontiguousarray(np.transpose(ab_gate_w2[0], (1, 0, 2)), dtype=np.float32)
    W["gate_b"] = np.ascontiguousarray(ab_gate_b[0].reshape(1, 2, 1024), dtype=np.float32)
    lg = np.asarray(ln_g, np.float32).reshape(4, 32, 128).transpose(2, 0, 1)
    lb = np.asarray(ln_b, np.float32).reshape(4, 32, 128).transpose(2, 0, 1)
    W["lng"] = np.ascontiguousarray(lg)
    W["lnb"] = np.ascontiguousarray(lb)
    ho = _head_order()
    W["sinks"] = np.ascontiguousarray(np.asarray(c_sinks[0], np.float32)[ho].reshape(1, 64))
    pk = np.arange(128)[:, None]
    q = np.arange(128)[None, :]
    tab = np.asarray(rel_bias_table, np.float32)
    bm = np.empty((128, 3, 64, 128), np.float32)
    for kb in range(3):
        rel = (kb - 1) * 128 + pk - q
        bk = _t5_bucket(rel)
        gathered = tab[bk]
        bm[:, kb] = np.transpose(gathered[:, :, ho], (0, 2, 1))
    W["bm"] = np.ascontiguousarray(bm.reshape(128, 3, 64 * 128))
    return W


_NC_CACHE = {}


def _run(NT, core_x, core_seqs, W):
    if NT not in _NC_CACHE:
        _NC_CACHE[NT] = build(NT)
    nc = _NC_CACHE[NT]
    in_maps = [_per_core_inputs(NT, core_x[c], core_seqs[c], W) for c in range(8)]
    res = run_bass_kernel_spmd(nc, in_maps, core_ids=list(range(8)))
    return [np.asarray(r["y"]) for r in res.results]


def kernel(x_prompt, x_sample, rel_bias_table, ab_w_in, ab_fourier_g, ab_gate_w2, ab_gate_b, ab_head_norm_g, ab_w_out,
           c_w_in, c_sinks, c_w_out, ffn_w1, ffn_w3, ffn_w2, ln_g, ln_b):
    x_prompt = np.asarray(x_prompt, np.float32)
    x_sample = np.asarray(x_sample, np.float32)
    B, S, _ = x_prompt.shape
    DB, DS, _ = x_sample.shape
    assert B == 8 and DB == 4 and 2 * S == DS
    NT = DS
    W = _shared_weight_inputs(rel_bias_table, ab_w_in, ab_fourier_g, ab_gate_w2, ab_gate_b, ab_head_norm_g, ab_w_out,
                              c_w_in, c_sinks, c_w_out, ffn_w1, ffn_w3, ffn_w2, ln_g, ln_b)
    core_x = []
    core_seqs = []
    for c in range(4):
        core_x.append(x_prompt[2 * c:2 * c + 2].reshape(NT, D))
        core_seqs.append([S, S])
    for c in range(4):
        core_x.append(x_sample[c])
        core_seqs.append([DS])
    ys = _run(NT, core_x, core_seqs, W)
    y_prompt = np.stack([ys[c].reshape(2, S, D) for c in range(4)]).reshape(B, S, D).astype(np.float32)
    y_sample = np.stack([ys[4 + c] for c in range(4)]).astype(np.float32)
    return (y_prompt, y_sample)
```

```python
import math
import os
from contextlib import ExitStack
import numpy as np
import ml_dtypes
import concourse.bass as bass
import concourse.mybir as mybir
from concourse.bass_utils import run_bass_kernel_spmd

F32 = mybir.dt.float32
BF16 = mybir.dt.bfloat16
AF = mybir.ActivationFunctionType
ALU = mybir.AluOpType

PE, ACT, DVE, POOL, SP = "pe", "act", "dve", "pool", "sp"
ENGS = [PE, ACT, DVE, POOL, SP]
N_DMA_SEMS = 48
N_POOL_SEMS = 16

D = 4096
DFF = 11008
NFF = 86
AB_IN = 8224
ALPHA = 4 ** 0.25
EPS = 1e-5
NEG = -30000.0


class Buf:
    __slots__ = ("w", "wl", "r", "rd", "x")

    def __init__(self, x=False):
        self.w = None
        self.wl = []
        self.r = {}
        self.rd = []
        self.x = x


def bufs(n):
    return [Buf() for _ in range(n)]


class Op:
    __slots__ = ("eng", "fn", "deps", "needed", "dma", "sem", "val", "ticket")

    def __init__(self, eng, fn, dma):
        self.eng = eng
        self.fn = fn
        self.deps = []
        self.needed = False
        self.dma = dma
        self.sem = None
        self.val = None
        self.ticket = 0


class Prog:
    def __init__(self):
        self.ops = {e: [] for e in ENGS}
        self.slot_last = [None] * N_DMA_SEMS
        self.slot_cnt = [0] * N_DMA_SEMS
        self.rr = 0
        self.rr_pool = 0
        self.stores = []
        self.pending = {}

    def barrier(self):
        lasts = []
        for e in ENGS:
            for op in reversed(self.ops[e]):
                if not op.dma:
                    lasts.append(op)
                    break
        lasts += [o for o in self.slot_last if o is not None]
        for e in ENGS:
            self.pending[e] = list(lasts)

    def add(self, eng, fn, reads=(), writes=(), dma=False, is_store=False):
        op = Op(eng, fn, dma)
        deps = {}

        def dep(d, kind):
            if d is None:
                return
            if (not d.dma) and (not dma) and d.eng == eng:
                if kind != "raw" or eng == PE:
                    return
            deps[id(d)] = d

        for b in reads:
            dep(b.w, "raw")
            for d_ in b.wl:
                dep(d_, "raw")
            if b.x:
                for r in b.r.values():
                    if r.eng != eng:
                        deps[id(r)] = r
        for b in writes:
            dep(b.w, "waw")
            for d_ in b.wl:
                dep(d_, "waw")
            for r in b.r.values():
                dep(r, "war")
            for r in b.rd:
                dep(r, "war")
        pend = self.pending.pop(eng, None)
        if pend:
            for d_ in pend:
                if d_.dma or d_.eng != eng:
                    deps[id(d_)] = d_
        if dma:
            if eng == POOL:
                s = self.rr_pool
                self.rr_pool = (self.rr_pool + 1) % N_POOL_SEMS
            else:
                s = N_POOL_SEMS + self.rr
                self.rr = (self.rr + 1) % (N_DMA_SEMS - N_POOL_SEMS)
            prev = self.slot_last[s]
            if prev is not None:
                deps[id(prev)] = prev
            self.slot_cnt[s] += 1
            op.sem = s
            op.val = 16 * self.slot_cnt[s]
            self.slot_last[s] = op
            if is_store:
                self.stores.append(op)
        op.deps = list(deps.values())
        for d_ in op.deps:
            d_.needed = True
        for b in reads:
            if dma:
                b.rd.append(op)
                if len(b.rd) > N_DMA_SEMS:
                    b.rd = b.rd[-N_DMA_SEMS:]
            else:
                b.r[eng] = op
        for b in writes:
            if dma:
                b.w = None
                b.wl = [op]
            else:
                b.w = op
                b.wl = []
            b.r = {}
            b.rd = []
        self.ops[eng].append(op)
        return op

    def emit(self, nc):
        with ExitStack() as st:
            esem = {e: st.enter_context(nc.semaphore("s_" + e)) for e in ENGS}
            dsem = [st.enter_context(nc.semaphore("d%d" % i)) for i in range(N_DMA_SEMS)]
            for e in ENGS:
                c = 0
                for op in self.ops[e]:
                    if not op.dma and op.needed:
                        c += 1
                        op.ticket = c
                if os.environ.get('K_VERBOSE'):
                    print("ENG", e, "ops", len(self.ops[e]), "tickets", c, flush=True)
            if os.environ.get('K_VERBOSE'):
                print("DMA slot max", max(self.slot_cnt) * 16, flush=True)
            final_waits = {}
            for op in self.stores:
                final_waits[op.sem] = max(final_waits.get(op.sem, 0), op.val)
            for op in self.slot_last:
                if op is not None:
                    final_waits[op.sem] = max(final_waits.get(op.sem, 0), op.val)
            block = st.enter_context(nc.Block())

            def run(e):
                def body(eng):
                    known = {}
                    for op in self.ops[e]:
                        w = {}
                        for d_ in op.deps:
                            if d_.dma:
                                k = ("d", d_.sem)
                                v = d_.val
                            else:
                                k = ("e", d_.eng)
                                v = d_.ticket
                            if v > w.get(k, 0):
                                w[k] = v
                        for k, v in w.items():
                            if known.get(k, 0) >= v:
                                continue
                            known[k] = v
                            eng.wait_ge(dsem[k[1]] if k[0] == "d" else esem[k[1]], v)
                        ins = op.fn(eng)
                        if op.dma:
                            ins.then_inc(dsem[op.sem], 16)
                        elif op.needed:
                            ins.then_inc(esem[e], 1)
                    if e == SP:
                        for s, v in final_waits.items():
                            if known.get(("d", s), 0) < v:
                                eng.wait_ge(dsem[s], v)
                return body

            block.tensor(run(PE))
            block.scalar(run(ACT))
            block.vector(run(DVE))
            block.gpsimd(run(POOL))
            block.sync(run(SP))


class Arena:
    def __init__(self, t, width):
        self.t = t
        self.width = width
        self.off = 0

    def reset(self, off=0):
        self.off = off

    def _take(self, words):
        a = self.off
        self.off += words
        assert self.off <= self.width, ("arena overflow", self.off, self.width)
        return a

    def f32(self, *shape):
        n = int(np.prod(shape))
        a = self._take(n)
        ap = self.t[:, a:a + n]
        return self._shape(ap, shape)

    def bf16(self, *shape):
        n = int(np.prod(shape))
        w = (n + 1) // 2
        a = self._take(w)
        ap = self.t[:, a:a + w].bitcast(BF16)
        if n != 2 * w:
            ap = ap[:, 0:n]
        return self._shape(ap, shape)

    @staticmethod
    def _shape(ap, shape):
        if len(shape) == 1:
            return ap
        if len(shape) == 2:
            return ap.rearrange("p (a b) -> p a b", a=shape[0], b=shape[1])
        if len(shape) == 3:
            return ap.rearrange("p (a b c) -> p a b c", a=shape[0], b=shape[1], c=shape[2])
        raise ValueError(shape)


class _Stop(Exception):
    pass


_HOLD = {}


def build(NT, stop=None, DFF=11008):
    try:
        return _build(NT, stop, DFF)
    except _Stop:
        return _HOLD["nc"]


def _build(NT, stop=None, DFF=11008):
    NFF = DFF // 128
    nc = bass.Bass("TRN2", target_bir_lowering=False)
    _HOLD["nc"] = nc

    def ck(k):
        if stop is not None and stop == k:
            p.emit(nc)
            raise _Stop()

    NCH = NT // 128
    NTT = NT // 512
    p = Prog()

    mini = stop is not None and stop <= 3
    BIGW = ("ab_w_out", "c_w_in", "c_w_out", "ffn_w1", "ffn_w3", "ffn_w2")

    def din(name, shape, dt=F32):
        if mini and name in BIGW:
            shape = [128, 128]
        return nc.dram_tensor(name, list(shape), dt, kind="ExternalInput").ap()

    def dscr(name, shape, dt):
        big = name.startswith("w") and name != "wabin_s"
        if mini and big:
            shape = [128, 128]
        kind = "ExternalOutput" if (stop is not None and not big and name != "wabin_s") else "Internal"
        return nc.dram_tensor(name, list(shape), dt, kind=kind).ap()

    x_d = din("x", [NT, D])
    wabin_d = din("ab_w_in", [D, AB_IN])
    wabout_d = din("ab_w_out", [D, D])
    wcin_d = din("c_w_in", [D, 5120])
    wcout_d = din("c_w_out", [D, D])
    if mini:
        w1_d = w3_d = w2_d = None
        din("ffn_w1", [1]); din("ffn_w3", [1]); din("ffn_w2", [1])
    else:
        w1_d = din("ffn_w1", [2, D, DFF])
        w3_d = din("ffn_w3", [2, D, DFF])
        w2_d = din("ffn_w2", [2, DFF, D])
    fg_d = din("fourier_g", [4, 512])
    hg_d = din("head_g", [4, 512])
    gw2_d = din("gate_w2", [16, 2, 1024])
    gb_d = din("gate_b", [1, 2, 1024])
    lng_d = din("lng", [128, 4, 32])
    lnb_d = din("lnb", [128, 4, 32])
    sinks_d = din("sinks", [1, 64])
    bm_d = din("bm", [128, 3, 64 * 128])
    maskadd_d = din("maskadd", [128, 3, 128])
    vflag_d = din("vflag", [128, NCH * 3])
    keep_d = din("keep", [128, 2 * (NCH + 1)])
    dftc_d = din("dftc", [NT, NT], BF16)
    dfts_d = din("dfts", [NT, NT], BF16)
    fch_d = din("fch", [128, 4, 1024], BF16)
    ident_d = din("ident", [128, 128])
    identb_d = din("identb", [128, 128], BF16)
    glam_d = din("glam", [128, 4, 128])
    smask_d = din("smask", [128, 2, 128])
    y_d = nc.dram_tensor("y", [NT, D], F32, kind="ExternalOutput").ap()

    wabin_s = dscr("wabin_s", [D, AB_IN], BF16)
    wabout_s = dscr("wabout_s", [D, D], BF16)
    wcin_s = dscr("wcin_s", [D, 5632], BF16)
    wcout_s = dscr("wcout_s", [D, D], BF16)
    w1_s = [dscr("w1_s%d" % l, [D, DFF], BF16) for l in range(2)]
    w3_s = [dscr("w3_s%d" % l, [D, DFF], BF16) for l in range(2)]
    w2_s = [dscr("w2_s%d" % l, [DFF, D], BF16) for l in range(2)]
    zres_s = dscr("zres_s", [D, NT], F32)
    un_s = dscr("un_s", [NT, 2048], BF16)
    qk_s = dscr("qk_s", [NT, 2048], F32)
    vv_s = dscr("vv_s", [NT, 2048], BF16)
    rr_s = dscr("rr_s", [NT, 2048], F32)
    sp_s = dscr("sp_s", [NT, 2048], F32)
    ob_s = dscr("ob_s", [NT, 2048], F32)
    mixT_s = dscr("mixT_s", [D, NT], BF16)
    qT_s = dscr("qT_s", [5120, NT], BF16)
    v2_s = dscr("v2_s", [NT, 1024], BF16)
    oT_s = dscr("oT_s", [D, NT], BF16)

    b_wabin = bufs(32); b_wabout = bufs(32); b_wcin = bufs(32); b_wcout = bufs(32)
    b_w1 = [bufs(32), bufs(32)]; b_w3 = [bufs(32), bufs(32)]; b_w2 = [bufs(NFF), bufs(NFF)]

    SKIP = os.environ.get('K_SKIP', '')

    def cast_rows(dst, src, bl, nblk):
        for rb in range(0 if 'c' in SKIP else nblk):
            p.add(POOL, lambda e, rb=rb: e.dma_start(out=dst[rb * 128:(rb + 1) * 128, :], in_=src[rb * 128:(rb + 1) * 128, :]),
                  writes=[bl[rb]], dma=True)

    cast_rows(wabin_s, wabin_d, b_wabin, 32)
    if not mini:
      cast_rows(wabout_s, wabout_d, b_wabout, 32)
    for l in range(0 if mini else 1):
        cast_rows(w1_s[l], w1_d[l], b_w1[l], 32)
        cast_rows(w3_s[l], w3_d[l], b_w3[l], 32)
        cast_rows(w2_s[l], w2_d[l], b_w2[l], NFF)
    for rb in range(0 if mini else 32):
        r0 = rb * 128
        p.add(POOL, lambda e, r0=r0: e.dma_start(out=wcin_s[r0:r0 + 128, 0:4096], in_=wcin_d[r0:r0 + 128, 0:4096]),
              writes=[b_wcin[rb]], dma=True)
    b_wcin2 = bufs(32); b_wcin3 = bufs(32); b_wcin4 = bufs(32)
    for rb in range(0 if mini else 32):
        r0 = rb * 128
        p.add(POOL, lambda e, r0=r0: e.dma_start(out=wcin_s[r0:r0 + 128, 5120:5632], in_=wcin_d[r0:r0 + 128, 4608:5120]),
              writes=[b_wcin2[rb]], dma=True)
        for u, bl in ((0, b_wcin3), (1, b_wcin4)):
            dst = wcin_s[r0:r0 + 128, 4096:5120].rearrange("r (g u d) -> r g u d", g=8, u=2, d=64)[:, :, u, :]
            src = wcin_d[r0:r0 + 128, 4096:4608].rearrange("r (g d) -> r g d", g=8, d=64)
            p.add(POOL, lambda e, dst=dst, src=src: e.dma_start(out=dst, in_=src), writes=[bl[rb]], dma=True)
    if not mini:
      cast_rows(wcout_s, wcout_d, b_wcout, 32)
    for l in range(1, 1 if mini else 2):
        cast_rows(w1_s[l], w1_d[l], b_w1[l], 32)
        cast_rows(w3_s[l], w3_d[l], b_w3[l], 32)
        cast_rows(w2_s[l], w2_d[l], b_w2[l], NFF)
    b_wcin_all = b_wcin + b_wcin2 + b_wcin3 + b_wcin4

    AW = 52600
    arena_t = nc.alloc_sbuf_tensor("arena", [128, AW], F32)
    ar = Arena(arena_t, AW)
    psum = [nc.alloc_psum_tensor("ps%d" % i, [128, 512], F32) for i in range(8)]
    b_ps = [Buf(True) for _ in range(8)]
    pcount = [0]

    def bank():
        i = pcount[0] % 8
        pcount[0] += 1
        return psum[i], b_ps[i]

    def mm(out_ap, pairs, reads, writes):
        def fn(e):
            n = len(pairs)
            ins = None
            for i, (l, r) in enumerate(pairs):
                ins = e.matmul(out_ap, l, r, start=(i == 0), stop=(i == n - 1))
            return ins
        return p.add(PE, fn, reads=reads, writes=writes)

    def load(out, in_, reads, writes, eng=SP):
        return p.add(eng, lambda e: e.dma_start(out=out, in_=in_), reads=reads, writes=writes, dma=True)

    def store(out, in_, reads, writes=(), eng=SP, final=False):
        return p.add(eng, lambda e: e.dma_start(out=out, in_=in_), reads=reads, writes=writes, dma=True, is_store=final)

    ident = ar.f32(128)
    identb = ar.bf16(128)
    ones_bf = ar.bf16(128)
    lng = ar.f32(4, 32); lnb = ar.f32(4, 32); lnga = ar.f32(4, 32); lnba = ar.f32(4, 32)
    b_const = Buf()
    load(ident, ident_d, [], [b_const])
    load(identb, identb_d, [], [b_const])
    if 'l' not in SKIP:
        load(lng, lng_d, [], [b_const])
        load(lnb, lnb_d, [], [b_const])
    p.add(DVE, lambda e: e.memset(ones_bf, 1.0), writes=[b_const])
    p.add(DVE, lambda e: e.tensor_scalar(out=lnga, in0=lng, scalar1=ALPHA, scalar2=None, op0=ALU.mult), reads=[b_const], writes=[b_const])
    p.add(DVE, lambda e: e.tensor_scalar(out=lnba, in0=lnb, scalar1=ALPHA, scalar2=None, op0=ALU.mult), reads=[b_const], writes=[b_const])
    BASE = ar.off
    b_scr = {k: Buf() for k in ["zres", "un", "qk", "vv", "rr", "sp", "ob", "mixT", "qT", "v2", "oT"]}

    ar.reset(BASE)
    xT = ar.bf16(32, 512); b_xT = Buf()
    xst = [ar.f32(4096) for _ in range(2)]; b_xst = bufs(2)
    Wt = [ar.bf16(32, 512) for _ in range(2)]; b_Wt = bufs(2)
    Wg = ar.bf16(32, 32); b_Wg = Buf()
    zst = [ar.f32(512) for _ in range(2)]; b_zst = bufs(2)
    fgt = ar.f32(4, 512); gw2 = ar.f32(2, 1024); gbt = ar.f32(2, 1024); ones_f = ar.f32(128)
    b_c1 = Buf()
    for g in range(0 if 'f' in SKIP else 4):
        load(fgt[:, g, :], fg_d[g:g + 1, :].partition_broadcast(128), [], [b_c1])
    if 'g' not in SKIP:
        load(gw2[0:16], gw2_d, [], [b_c1])
        load(gbt[0:1], gb_d, [], [b_c1])
    p.add(DVE, lambda e: e.memset(ones_f, 1.0), writes=[b_c1])
    ost32 = [ar.f32(512) for _ in range(3)]; b_o32 = bufs(3)
    ostbf = [ar.bf16(512) for _ in range(3)]; b_obf = bufs(3)
    tmp32 = [ar.f32(512) for _ in range(2)]; b_t32 = bufs(2)
    stat = [ar.f32(8) for _ in range(4)]; b_stat = bufs(4)
    lrT = ar.f32(2, 512); b_lrT = Buf()
    cnt = {"w": 0, "o32": 0, "obf": 0, "t32": 0, "st": 0, "zst": 0}

    def rot(key, n):
        i = cnt[key] % n
        cnt[key] += 1
        return i

    def gn_stats(src_ap, b_src, width, st, b_st, junk, b_junk):
        p.add(ACT, lambda e: e.activation(out=junk, in_=src_ap, func=AF.Copy, accum_out=st[:, 0:1]), reads=[b_src], writes=[b_junk, b_st])
        p.add(ACT, lambda e: e.activation(out=junk, in_=src_ap, func=AF.Square, accum_out=st[:, 1:2]), reads=[b_src], writes=[b_junk, b_st])
        p.add(DVE, lambda e: e.tensor_scalar(out=st[:, 2:3], in0=st[:, 0:1], scalar1=1.0 / width, scalar2=None, op0=ALU.mult), reads=[b_st], writes=[b_st])
        p.add(DVE, lambda e: e.tensor_tensor(out=st[:, 3:4], in0=st[:, 2:3], in1=st[:, 2:3], op=ALU.mult), reads=[b_st], writes=[b_st])
        p.add(DVE, lambda e: e.scalar_tensor_tensor(out=st[:, 4:5], in0=st[:, 1:2], scalar=1.0 / width, in1=st[:, 3:4], op0=ALU.mult, op1=ALU.subtract), reads=[b_st], writes=[b_st])
        p.add(DVE, lambda e: e.tensor_scalar(out=st[:, 6:7], in0=st[:, 4:5], scalar1=EPS, scalar2=None, op0=ALU.add), reads=[b_st], writes=[b_st])
        p.add(ACT, lambda e: e.activation(out=st[:, 7:8], in_=st[:, 6:7], func=AF.Sqrt), reads=[b_st], writes=[b_st])
        p.add(DVE, lambda e: e.reciprocal(out=st[:, 5:6], in_=st[:, 7:8]), reads=[b_st], writes=[b_st])

    CUT = int(os.environ.get('K_CUT', '9'))
    b_ser = Buf()
    for tt in range(NTT if CUT >= 2 else 0):
        t0 = tt * 512
        for ts in range(4):
            xi = rot("w", 2)
            load(xst[xi], x_d[t0 + ts * 128:t0 + (ts + 1) * 128, :], [], [b_xst[xi]])
            for kq in range(8):
                ps, bp = bank()
                def fn(e, ps=ps, xi=xi, kq=kq):
                    ins = None
                    for k4 in range(4):
                        kc = kq * 4 + k4
                        ins = e.transpose(ps[:, k4 * 128:(k4 + 1) * 128], xst[xi][:, kc * 128:(kc + 1) * 128], ident)
                    return ins
                p.add(PE, fn, reads=[b_xst[xi], b_const], writes=[bp])
                p.add(ACT, lambda e, ps=ps, kq=kq, ts=ts: e.activation(
                    out=xT[:, kq * 4:(kq + 1) * 4, ts * 128:(ts + 1) * 128],
                    in_=ps[:].rearrange("p (a b) -> p a b", a=4, b=128), func=AF.Copy), reads=[bp], writes=[b_xT, b_ser])
                zi = rot("zst", 2)
                p.add(DVE, lambda e, ps=ps, zi=zi: e.tensor_scalar(out=zst[zi], in0=ps[:], scalar1=ALPHA, scalar2=None, op0=ALU.mult),
                      reads=[bp, b_ser], writes=[b_zst[zi]])
                dst = zres_s[kq * 512:(kq + 1) * 512, t0 + ts * 128:t0 + (ts + 1) * 128].rearrange("(a p) t -> p a t", p=128)
                store(dst, zst[zi].rearrange("p (a b) -> p a b", a=4, b=128), [b_zst[zi]], [b_scr["zres"]])
        if CUT < 3:
            continue
        load(Wg, wabin_s[:, 8192:8224].rearrange("(kc p) n -> p kc n", p=128), b_wabin, [b_Wg])
        for d_ in range(2):
            ps, bp = bank()
            mm(ps[0:16, :], [(Wg[:, kc, d_ * 16:(d_ + 1) * 16], xT[:, kc, :]) for kc in range(32)], [b_Wg, b_xT], [bp])
            p.add(ACT, lambda e, ps=ps, d_=d_: e.activation(out=lrT[0:16, d_, :], in_=ps[0:16, :], func=AF.Copy), reads=[bp], writes=[b_lrT])
        for ts in range(4):
            for d_ in range(2):
                for hf in range(2):
                    ps, bp = bank()
                    mm(ps[:], [(lrT[0:16, d_, ts * 128:(ts + 1) * 128], gw2[0:16, d_, hf * 512:(hf + 1) * 512]),
                               (ones_f[0:1, :], gbt[0:1, d_, hf * 512:(hf + 1) * 512])], [b_lrT, b_c1], [bp])
                    oi = rot("o32", 3)
                    p.add(ACT, lambda e, ps=ps, oi=oi: e.activation(out=ost32[oi], in_=ps[:], func=AF.Exp, scale=-1.0), reads=[bp], writes=[b_o32[oi]])
                    p.add(ACT, lambda e, oi=oi: e.activation(out=ost32[oi], in_=ost32[oi], func=AF.Ln, bias=1.0), reads=[b_o32[oi]], writes=[b_o32[oi]])
                    store(sp_s[t0 + ts * 128:t0 + (ts + 1) * 128, d_ * 1024 + hf * 512:d_ * 1024 + (hf + 1) * 512], ost32[oi],
                          [b_o32[oi]], [b_scr["sp"]])
        if CUT < 4:
            continue
        def wt_load(ct_):
            wi_ = ct_ % 2
            load(Wt[wi_], wabin_s[:, ct_ * 512:(ct_ + 1) * 512].rearrange("(kc p) n -> p kc n", p=128), b_wabin, [b_Wt[wi_]])
        wt_load(0)
        for ct in range(16):
            wi = ct % 2
            if ct + 1 < 16:
                wt_load(ct + 1)
            for ts in range(4):
                ps, bp = bank()
                mm(ps[:], [(xT[:, kc, ts * 128:(ts + 1) * 128], Wt[wi][:, kc, :]) for kc in range(32)], [b_xT, b_Wt[wi]], [bp])
                rows = slice(t0 + ts * 128, t0 + (ts + 1) * 128)
                if ct < 4:
                    si = rot("st", 4); ti = rot("t32", 2); oi = rot("obf", 3)
                    gn_stats(ps[:], bp, 512.0, stat[si], b_stat[si], tmp32[ti], b_t32[ti])
                    p.add(DVE, lambda e, ps=ps, si=si, ti=ti: e.tensor_scalar(out=tmp32[ti], in0=ps[:], scalar1=stat[si][:, 2:3], scalar2=stat[si][:, 5:6],
                                                                         op0=ALU.subtract, op1=ALU.mult), reads=[bp, b_stat[si]], writes=[b_t32[ti]])
                    p.add(DVE, lambda e, ti=ti, oi=oi, ct=ct: e.tensor_tensor(out=ostbf[oi], in0=tmp32[ti], in1=fgt[:, ct, :], op=ALU.mult),
                          reads=[b_t32[ti], b_c1], writes=[b_obf[oi]])
                    store(un_s[rows, ct * 512:(ct + 1) * 512], ostbf[oi], [b_obf[oi]], [b_scr["un"]])
                elif ct < 8:
                    oi = rot("o32", 3)
                    p.add(ACT, lambda e, ps=ps, oi=oi: e.activation(out=ost32[oi], in_=ps[:], func=AF.Copy), reads=[bp], writes=[b_o32[oi]])
                    store(qk_s[rows, (ct - 4) * 512:(ct - 3) * 512], ost32[oi], [b_o32[oi]], [b_scr["qk"]])
                elif ct < 12:
                    oi = rot("obf", 3)
                    p.add(ACT, lambda e, ps=ps, oi=oi: e.activation(out=ostbf[oi], in_=ps[:], func=AF.Copy), reads=[bp], writes=[b_obf[oi]])
                    store(vv_s[rows, (ct - 8) * 512:(ct - 7) * 512], ostbf[oi], [b_obf[oi]], [b_scr["vv"]])
                else:
                    oi = rot("o32", 3)
                    p.add(DVE, lambda e, ps=ps, oi=oi: e.tensor_copy(out=ost32[oi], in_=ps[:]), reads=[bp], writes=[b_o32[oi]])
                    store(rr_s[rows, (ct - 12) * 512:(ct - 11) * 512], ost32[oi], [b_o32[oi]], [b_scr["rr"]])
    p.barrier()
    ck(1)

    ar.reset(BASE)
    NSC = NCH
    Cs = [ar.bf16(NSC, 512) for _ in range(2)]; Ss = [ar.bf16(NSC, 512) for _ in range(2)]
    b_Cs = bufs(2); b_Ss = bufs(2)
    ung = ar.bf16(NSC, 512); b_ung = Buf()
    fch = ar.bf16(4, 1024); b_fch = Buf()
    QT = [ar.bf16(8, 512) for _ in range(2)]; b_QT = bufs(2)
    fo = [ar.bf16(512) for _ in range(3)]; b_fo = bufs(3)
    load(fch, fch_d, [], [b_fch])
    cq = 0; cf = 0; cs_i = 0
    for g in range(4):
        load(ung, un_s[:, g * 512:(g + 1) * 512].rearrange("(sc p) c -> p sc c", p=128), [b_scr["un"]], [b_ung])
        for st_ in range(NTT):
            ci = cs_i % 2; cs_i += 1
            load(Cs[ci], dftc_d[:, st_ * 512:(st_ + 1) * 512].rearrange("(sc p) s -> p sc s", p=128), [], [b_Cs[ci]])
            load(Ss[ci], dfts_d[:, st_ * 512:(st_ + 1) * 512].rearrange("(sc p) s -> p sc s", p=128), [], [b_Ss[ci]])
            qi = cq % 2; cq += 1
            for part, (Tb, bT) in enumerate(((Cs[ci], b_Cs[ci]), (Ss[ci], b_Ss[ci]))):
                for cb in range(4):
                    ps, bp = bank()
                    mm(ps[:], [(ung[:, sc, cb * 128:(cb + 1) * 128], Tb[:, sc, :]) for sc in range(NSC)], [b_ung, bT], [bp])
                    eng = ACT if cb % 2 == 0 else DVE
                    if eng == ACT:
                        p.add(ACT, lambda e, ps=ps, qi=qi, part=part, cb=cb: e.activation(out=QT[qi][:, part * 4 + cb, :], in_=ps[:], func=AF.Copy),
                              reads=[bp], writes=[b_QT[qi]])
                    else:
                        p.add(DVE, lambda e, ps=ps, qi=qi, part=part, cb=cb: e.tensor_copy(out=QT[qi][:, part * 4 + cb, :], in_=ps[:]),
                              reads=[bp], writes=[b_QT[qi]])
            for cpb in range(4):
                ps, bp = bank()
                pairs = []
                for part in range(2):
                    for cb in range(4):
                        pairs.append((fch[:, cb, part * 512 + cpb * 128:part * 512 + (cpb + 1) * 128], QT[qi][:, part * 4 + cb, :]))
                mm(ps[:], pairs, [b_fch, b_QT[qi]], [bp])
                fi = cf % 3; cf += 1
                p.add(ACT, lambda e, ps=ps, fi=fi: e.activation(out=fo[fi], in_=ps[:], func=AF.Copy), reads=[bp], writes=[b_fo[fi]])
                r0 = g * 512 + cpb * 128
                store(mixT_s[r0:r0 + 128, st_ * 512:(st_ + 1) * 512], fo[fi], [b_fo[fi]], [b_scr["mixT"]])
    p.barrier()
    ck(2)

    ar.reset(BASE)
    glam = ar.f32(4, 128); smask = ar.f32(2, 128); keep = ar.f32(2 * (NCH + 1)); hgt = ar.f32(4, 512)
    negs = ar.f32(1)
    b_c2 = Buf()
    load(glam, glam_d, [], [b_c2]); load(smask, smask_d, [], [b_c2]); load(keep, keep_d, [], [b_c2])
    for h in range(4):
        load(hgt[:, h, :], hg_d[h:h + 1, :].partition_broadcast(128), [], [b_c2])
    p.add(DVE, lambda e: e.memset(negs, -1.0 / 16), writes=[b_c2])
    S32 = ar.f32(8, 512); Sbf = ar.bf16(8, 512); b_S32 = bufs(8); b_Sbf = bufs(8)
    inq = [ar.f32(1024) for _ in range(2)]; ink = [ar.f32(1024) for _ in range(2)]
    inv = [ar.bf16(2048) for _ in range(2)]; insp = [ar.f32(1024) for _ in range(2)]
    inr = [ar.f32(2048) for _ in range(2)]; inob = [ar.f32(2048) for _ in range(2)]
    b_in = [bufs(6) for _ in range(2)]
    e1 = ar.f32(1024); b_e1 = Buf()
    e2 = ar.f32(1024); b_e2 = Buf()
    qg = ar.bf16(1024); kg = ar.bf16(1024); kdec = ar.bf16(1024); b_qg = Buf(); b_kg = Buf(); b_kdec = Buf()
    qgT = ar.bf16(8, 128); kgT = ar.bf16(8, 128); b_qgT = Buf(); b_kgT = Buf()
    dec8 = ar.f32(8); b_dec = Buf()
    sTm = [ar.bf16(128) for _ in range(2)]; b_sTm = bufs(2)
    o32 = [ar.f32(512) for _ in range(2)]; b_o32g = bufs(2)
    t32 = [ar.f32(512) for _ in range(2)]; b_t32g = bufs(2)
    sil = [ar.f32(512) for _ in range(2)]; b_sil = bufs(2)
    bo = [ar.bf16(512) for _ in range(2)]; b_bo = bufs(2)
    boT = [ar.bf16(4, 128) for _ in range(2)]; b_boT = bufs(2)
    gst = [ar.f32(8) for _ in range(4)]; b_gst = bufs(4)
    gc = {"in": 0, "s": 0, "o": 0, "st": 0}

    for dirn in (1, 0):
        for i in range(8):
            p.add(POOL, lambda e, i=i: e.memset(S32[:, i, :], 0.0), writes=[b_S32[i]])
            p.add(POOL, lambda e, i=i: e.memset(Sbf[:, i, :], 0.0), writes=[b_Sbf[i]])
        order = list(range(NCH)) if dirn == 0 else list(range(NCH - 1, -1, -1))
        for ci_, c in enumerate(order):
            nxt = order[ci_ + 1] if ci_ + 1 < NCH else NCH
            rows = slice(c * 128, (c + 1) * 128)
            ii = gc["in"] % 2; gc["in"] += 1
            bi = b_in[ii]
            load(inq[ii], qk_s[rows, 0:1024], [b_scr["qk"]], [bi[0]])
            load(ink[ii], qk_s[rows, 1024:2048], [b_scr["qk"]], [bi[1]])
            load(inv[ii], vv_s[rows, :], [b_scr["vv"]], [bi[2]])
            load(insp[ii], sp_s[rows, dirn * 1024:(dirn + 1) * 1024], [b_scr["sp"]], [bi[3]])
            if dirn == 0:
                load(inr[ii], rr_s[rows, :], [b_scr["rr"]], [bi[4]])
                load(inob[ii], ob_s[rows, :], [b_scr["ob"]], [bi[5]])
            for hf in range(2):
                ps, bp = bank()
                mm(ps[:], [(glam[:, dirn * 2, :], insp[ii][:, hf * 512:(hf + 1) * 512])], [b_c2, bi[3]], [bp])
                sl = slice(hf * 512, (hf + 1) * 512)
                p.add(ACT, lambda e, ps=ps, sl=sl: e.activation(out=e1[:, sl], in_=ps[:], func=AF.Exp), reads=[bp], writes=[b_e1])
                p.add(ACT, lambda e, ps=ps, sl=sl: e.activation(out=e2[:, sl], in_=ps[:], func=AF.Exp, scale=-1.0), reads=[bp], writes=[b_e2])
            p.add(DVE, lambda e, ii=ii: e.scalar_tensor_tensor(out=qg, in0=inq[ii], scalar=1.0 / 16, in1=e1, op0=ALU.mult, op1=ALU.mult),
                  reads=[bi[0], b_e1], writes=[b_qg])
            p.add(POOL, lambda e, ii=ii: e.tensor_tensor(out=kg, in0=ink[ii], in1=e2, op=ALU.mult), reads=[bi[1], b_e2], writes=[b_kg])
            for hf in range(2):
                ps, bp = bank()
                mm(ps[:], [(glam[:, dirn * 2 + 1, :], insp[ii][:, hf * 512:(hf + 1) * 512])], [b_c2, bi[3]], [bp])
                sl = slice(hf * 512, (hf + 1) * 512)
                p.add(ACT, lambda e, ps=ps, sl=sl: e.activation(out=e1[:, sl], in_=ps[:], func=AF.Exp), reads=[bp], writes=[b_e1])
            p.add(DVE, lambda e, ii=ii: e.tensor_tensor(out=kdec, in0=ink[ii], in1=e1, op=ALU.mult), reads=[bi[1], b_e1], writes=[b_kdec])
            ps, bp = bank()
            def fn(e, ps=ps, ii=ii):
                ins = None
                for i in range(8):
                    ins = e.matmul(ps[:, i:i + 1], insp[ii][:, i * 128:(i + 1) * 128], negs, start=True, stop=True)
                return ins
            p.add(PE, fn, reads=[bi[3], b_c2], writes=[bp])
            p.add(ACT, lambda e, ps=ps: e.activation(out=dec8, in_=ps[:, 0:8], func=AF.Exp), reads=[bp], writes=[b_dec])
            kcol = dirn * (NCH + 1) + c
            p.add(DVE, lambda e, kcol=kcol: e.tensor_scalar(out=dec8, in0=dec8, scalar1=keep[:, kcol:kcol + 1], scalar2=None, op0=ALU.mult),
                  reads=[b_dec, b_c2], writes=[b_dec])
            for src, b_src, dstT, b_dstT in ((qg, b_qg, qgT, b_qgT), (kg, b_kg, kgT, b_kgT)):
                ps, bp = bank()
                psb = ps[:].bitcast(BF16)
                def fn(e, psb=psb, src=src):
                    ins = None
                    for i in range(8):
                        ins = e.transpose(psb[:, i * 128:(i + 1) * 128], src[:, i * 128:(i + 1) * 128], identb)
                    return ins
                p.add(PE, fn, reads=[b_src, b_const], writes=[bp])
                p.add(ACT, lambda e, psb=psb, dstT=dstT: e.activation(out=dstT, in_=psb.rearrange("p (a b) -> p a b", a=8, b=128), func=AF.Copy),
                      reads=[bp], writes=[b_dstT])
            for h in range(4):
                vh = inv[ii][:, h * 512:(h + 1) * 512]
                ps, bp = bank()
                mm(ps[:, 0:128], [(kgT[:, 2 * h + dc, :], qgT[:, 2 * h + dc, :]) for dc in range(2)], [b_kgT, b_qgT], [bp])
                si = gc["s"] % 2; gc["s"] += 1
                p.add(DVE, lambda e, ps=ps, si=si, dirn=dirn: e.tensor_tensor(out=sTm[si], in0=ps[:, 0:128], in1=smask[:, dirn, :], op=ALU.mult),
                      reads=[bp, b_c2], writes=[b_sTm[si]])
                ps, bp = bank()
                mm(ps[:], [(sTm[si], vh)] + [(qgT[:, 2 * h + dc, :], Sbf[:, 2 * h + dc, :]) for dc in range(2)],
                   [b_sTm[si], bi[2], b_qgT, b_Sbf[2 * h], b_Sbf[2 * h + 1]], [bp])
                oi = gc["o"] % 2; gc["o"] += 1
                hs = slice(h * 512, (h + 1) * 512)
                if dirn == 1:
                    p.add(ACT, lambda e, ps=ps, oi=oi: e.activation(out=o32[oi], in_=ps[:], func=AF.Copy), reads=[bp], writes=[b_o32g[oi]])
                    store(ob_s[rows, hs], o32[oi], [b_o32g[oi]], [b_scr["ob"]])
                else:
                    p.add(DVE, lambda e, ps=ps, oi=oi, ii=ii, hs=hs: e.tensor_tensor(out=o32[oi], in0=ps[:], in1=inob[ii][:, hs], op=ALU.add),
                          reads=[bp, bi[5]], writes=[b_o32g[oi]])
                    gi = gc["st"] % 4; gc["st"] += 1
                    gn_stats(o32[oi], b_o32g[oi], 512.0, gst[gi], b_gst[gi], t32[oi], b_t32g[oi])
                    p.add(DVE, lambda e, oi=oi, gi=gi: e.tensor_scalar(out=t32[oi], in0=o32[oi], scalar1=gst[gi][:, 2:3], scalar2=gst[gi][:, 5:6],
                                                                      op0=ALU.subtract, op1=ALU.mult), reads=[b_o32g[oi], b_gst[gi]], writes=[b_t32g[oi]])
                    p.add(POOL, lambda e, oi=oi, h=h: e.tensor_tensor(out=t32[oi], in0=t32[oi], in1=hgt[:, h, :], op=ALU.mult),
                          reads=[b_t32g[oi], b_c2], writes=[b_t32g[oi]])
                    p.add(ACT, lambda e, oi=oi, ii=ii, hs=hs: e.activation(out=sil[oi], in_=inr[ii][:, hs], func=AF.Silu), reads=[bi[4]], writes=[b_sil[oi]])
                    p.add(DVE, lambda e, oi=oi: e.tensor_tensor(out=bo[oi], in0=t32[oi], in1=sil[oi], op=ALU.mult),
                          reads=[b_t32g[oi], b_sil[oi]], writes=[b_bo[oi]])
                    ps2, bp2 = bank()
                    psb = ps2[:, 0:256].bitcast(BF16)
                    def fn(e, psb=psb, oi=oi):
                        ins = None
                        for i in range(4):
                            ins = e.transpose(psb[:, i * 128:(i + 1) * 128], bo[oi][:, i * 128:(i + 1) * 128], identb)
                        return ins
                    p.add(PE, fn, reads=[b_bo[oi], b_const], writes=[bp2])
                    p.add(ACT, lambda e, psb=psb, oi=oi: e.activation(out=boT[oi], in_=psb.rearrange("p (a b) -> p a b", a=4, b=128), func=AF.Copy),
                          reads=[bp2], writes=[b_boT[oi]])
                    r0 = 2048 + h * 512
                    store(mixT_s[r0:r0 + 512, rows].rearrange("(a p) t -> p a t", p=128), boT[oi], [b_boT[oi]], [b_scr["mixT"]])
                for dc in range(2):
                    i8 = 2 * h + dc
                    ps, bp = bank()
                    mm(ps[:], [(kdec[:, i8 * 128:(i8 + 1) * 128], vh)], [b_kdec, bi[2]], [bp])
                    p.add(DVE, lambda e, ps=ps, i8=i8: e.scalar_tensor_tensor(out=S32[:, i8, :], in0=S32[:, i8, :], scalar=dec8[:, i8:i8 + 1], in1=ps[:],
                                                                           op0=ALU.mult, op1=ALU.add), reads=[bp, b_dec, b_S32[i8]], writes=[b_S32[i8]])
                    kn = dirn * (NCH + 1) + nxt
                    p.add(ACT, lambda e, i8=i8, kn=kn: e.activation(out=Sbf[:, i8, :], in_=S32[:, i8, :], func=AF.Copy, scale=keep[:, kn:kn + 1]),
                          reads=[b_S32[i8], b_c2], writes=[b_Sbf[i8]])
    p.barrier()
    ck(3)

    def dense_chain(layer, srcT_s, b_src, win_s, b_win, final):
        ar.reset(BASE)
        xTc = ar.bf16(32, 512); b_x = bufs(32)
        z = ar.f32(32, 512); b_z = bufs(32)
        GB = 9
        aT = ar.bf16(GB, 512); b_aT = bufs(GB)
        WR0 = ar.off
        NWA, NWB = 4, 7
        Wa = [ar.bf16(32, 256) for _ in range(NWA)]
        ar.reset(WR0)
        Wb = [ar.bf16(GB, 512) for _ in range(NWB)]
        ar.reset(WR0)
        Wv = ar.bf16(32, 512)
        ext = {}
        for i_ in range(NWA):
            ext[("a", i_)] = (4096 * i_, 4096 * (i_ + 1))
        for i_ in range(NWB):
            ext[("b", i_)] = (GB * 256 * i_, GB * 256 * (i_ + 1))
        ext[("v", 0)] = (0, 8192)
        pts = sorted(set(x_ for se in ext.values() for x_ in se))
        regs = [((pts[i_], pts[i_ + 1]), Buf()) for i_ in range(len(pts) - 1)]
        cover = {k_: [b_ for (lo, hi), b_ in regs if lo >= se[0] and hi <= se[1]] for k_, se in ext.items()}
        b_Wa = [cover[("a", i_)] for i_ in range(NWA)]
        b_Wb = [cover[("b", i_)] for i_ in range(NWB)]
        b_Wv = cover[("v", 0)]
        ar.reset(WR0 + 16384)
        zb = [ar.bf16(512) for _ in range(2)]; zq = [ar.bf16(512) for _ in range(2)]; b_zb = bufs(2); b_zq = bufs(2)
        mean = ar.f32(512); rstd = ar.f32(512); msq = ar.f32(512); b_mr = Buf()
        tl = [ar.f32(512) for _ in range(2)]; b_tl = bufs(2)
        sl_ = [ar.f32(512) for _ in range(2)]; b_sl = bufs(2)
        qst = [ar.bf16(512) for _ in range(3)]; b_qst = bufs(3)
        vst = [ar.bf16(1024) for _ in range(2)]; b_vst = bufs(2)
        yst = [ar.f32(512) for _ in range(2)]; b_yst = bufs(2)
        cc = {"wa": 0, "wb": 0, "zb": 0, "tl": 0, "sl": 0, "q": 0, "v": 0, "y": 0}

        def layer_norm(lnidx, is_final, tok0):
            ps1, bp1 = bank()
            ps2, bp2 = bank()
            his = []
            for kc in range(32):
                i = cc["zb"] % 2; cc["zb"] += 1
                p.add(ACT, lambda e, kc=kc, i=i: e.activation(out=zb[i], in_=z[:, kc, :], func=AF.Copy), reads=[b_z[kc]], writes=[b_zb[i]])
                p.add(DVE, lambda e, kc=kc, i=i: e.tensor_tensor(out=zq[i], in0=z[:, kc, :], in1=z[:, kc, :], op=ALU.mult), reads=[b_z[kc]], writes=[b_zq[i]])
                p.add(PE, lambda e, kc=kc, i=i, ps1=ps1: e.matmul(ps1[:], ones_bf, zb[i], start=(kc == 0), stop=(kc == 31)),
                      reads=[b_zb[i], b_const], writes=[bp1])
                p.add(PE, lambda e, kc=kc, i=i, ps2=ps2: e.matmul(ps2[:], ones_bf, zq[i], start=(kc == 0), stop=(kc == 31)),
                      reads=[b_zq[i], b_const], writes=[bp2])
            p.add(DVE, lambda e: e.tensor_scalar(out=mean, in0=ps1[:], scalar1=1.0 / D, scalar2=None, op0=ALU.mult), reads=[bp1], writes=[b_mr])
            p.add(DVE, lambda e: e.tensor_tensor(out=msq, in0=mean, in1=mean, op=ALU.mult), reads=[b_mr], writes=[b_mr])
            p.add(DVE, lambda e: e.scalar_tensor_tensor(out=rstd, in0=ps2[:], scalar=1.0 / D, in1=msq, op0=ALU.mult, op1=ALU.subtract), reads=[bp2, b_mr], writes=[b_mr])
            p.add(DVE, lambda e: e.tensor_scalar(out=rstd, in0=rstd, scalar1=EPS, scalar2=None, op0=ALU.add), reads=[b_mr], writes=[b_mr])
            p.add(ACT, lambda e: e.activation(out=rstd, in_=rstd, func=AF.Sqrt), reads=[b_mr], writes=[b_mr])
            p.add(DVE, lambda e: e.reciprocal(out=rstd, in_=rstd), reads=[b_mr], writes=[b_mr])
            for kc in range(32):
                i = cc["tl"] % 2; cc["tl"] += 1
                p.add(DVE, lambda e, kc=kc, i=i: e.tensor_tensor(out=tl[i], in0=z[:, kc, :], in1=mean, op=ALU.subtract), reads=[b_z[kc], b_mr], writes=[b_tl[i]])
                p.add(POOL, lambda e, i=i: e.tensor_tensor(out=tl[i], in0=tl[i], in1=rstd, op=ALU.mult), reads=[b_tl[i], b_mr], writes=[b_tl[i]])
                if not is_final:
                    p.add(ACT, lambda e, kc=kc, i=i: e.activation(out=xTc[:, kc, :], in_=tl[i], func=AF.Identity,
                                                                 scale=lng[:, lnidx, kc:kc + 1], bias=lnb[:, lnidx, kc:kc + 1]),
                          reads=[b_tl[i], b_const], writes=[b_x[kc]])
                    p.add(ACT, lambda e, kc=kc, i=i: e.activation(out=z[:, kc, :], in_=tl[i], func=AF.Identity,
                                                                 scale=lnga[:, lnidx, kc:kc + 1], bias=lnba[:, lnidx, kc:kc + 1]),
                          reads=[b_tl[i], b_const], writes=[b_z[kc]])
                else:
                    p.add(ACT, lambda e, kc=kc, i=i: e.activation(out=z[:, kc, :], in_=tl[i], func=AF.Identity,
                                                                 scale=lng[:, lnidx, kc:kc + 1], bias=lnb[:, lnidx, kc:kc + 1]),
                          reads=[b_tl[i], b_const], writes=[b_z[kc]])
                    ps, bp = bank()
                    def fn(e, ps=ps, kc=kc):
                        ins = None
                        for ts in range(4):
                            ins = e.transpose(ps[:, ts * 128:(ts + 1) * 128], z[:, kc, ts * 128:(ts + 1) * 128], ident)
                        return ins
                    p.add(PE, fn, reads=[b_z[kc], b_const], writes=[bp])
                    yi = cc["y"] % 2; cc["y"] += 1
                    p.add(DVE, lambda e, ps=ps, yi=yi: e.tensor_copy(out=yst[yi], in_=ps[:]), reads=[bp], writes=[b_yst[yi]])
                    dst = y_d[tok0:tok0 + 512, kc * 128:(kc + 1) * 128].rearrange("(a p) f -> p a f", p=128)
                    p.add(SP, lambda e, dst=dst, yi=yi: e.dma_start(out=dst, in_=yst[yi].rearrange("p (a b) -> p a b", a=4, b=128)),
                          reads=[b_yst[yi]], dma=True, is_store=True)

        def proj_res(w_s, b_w):
            for ip in range(16):
                wi = cc["wa"] % NWA; cc["wa"] += 1
                load(Wa[wi], w_s[:, ip * 256:(ip + 1) * 256].rearrange("(kc p) n -> p kc n", p=128), b_w, b_Wa[wi])
                for i2 in range(2):
                    i = ip * 2 + i2
                    ps, bp = bank()
                    mm(ps[:], [(Wa[wi][:, kc, i2 * 128:(i2 + 1) * 128], xTc[:, kc, :]) for kc in range(32)], b_Wa[wi] + b_x, [bp])
                    p.add(DVE, lambda e, ps=ps, i=i: e.tensor_tensor(out=z[:, i, :], in0=z[:, i, :], in1=ps[:], op=ALU.add), reads=[bp, b_z[i]], writes=[b_z[i]])

        def ffn(l):
            ng = (NFF + GB - 1) // GB
            groups = [((NFF * gi) // ng, (NFF * (gi + 1)) // ng) for gi in range(ng)]
            for (j0, j1) in groups:
                nj = j1 - j0
                jj = j0
                while jj < j1:
                    nb = min(2, j1 - jj)
                    w1i = cc["wa"] % NWA; cc["wa"] += 1
                    load(Wa[w1i][:, :, 0:nb * 128], w1_s[l][:, jj * 128:(jj + nb) * 128].rearrange("(kc p) n -> p kc n", p=128), b_w1[l], b_Wa[w1i])
                    w3i = cc["wa"] % NWA; cc["wa"] += 1
                    load(Wa[w3i][:, :, 0:nb * 128], w3_s[l][:, jj * 128:(jj + nb) * 128].rearrange("(kc p) n -> p kc n", p=128), b_w3[l], b_Wa[w3i])
                    for b2 in range(nb):
                        j = jj + b2
                        ps1, bp1 = bank()
                        mm(ps1[:], [(Wa[w1i][:, kc, b2 * 128:(b2 + 1) * 128], xTc[:, kc, :]) for kc in range(32)], b_Wa[w1i] + b_x, [bp1])
                        ps3, bp3 = bank()
                        mm(ps3[:], [(Wa[w3i][:, kc, b2 * 128:(b2 + 1) * 128], xTc[:, kc, :]) for kc in range(32)], b_Wa[w3i] + b_x, [bp3])
                        si = cc["sl"] % 2; cc["sl"] += 1
                        p.add(ACT, lambda e, ps1=ps1, si=si: e.activation(out=sl_[si], in_=ps1[:], func=AF.Silu), reads=[bp1], writes=[b_sl[si]])
                        p.add(DVE, lambda e, ps3=ps3, si=si, jl=j - j0: e.tensor_tensor(out=aT[:, jl, :], in0=sl_[si], in1=ps3[:], op=ALU.mult),
                              reads=[bp3, b_sl[si]], writes=[b_aT[j - j0]])
                    jj += nb
                for ib in range(8):
                    wi = cc["wb"] % NWB; cc["wb"] += 1
                    load(Wb[wi][:, 0:nj, :], w2_s[l][j0 * 128:j1 * 128, ib * 512:(ib + 1) * 512].rearrange("(j p) n -> p j n", p=128),
                         b_w2[l][j0:j1], b_Wb[wi])
                    for i4 in range(4):
                        i = ib * 4 + i4
                        ps, bp = bank()
                        mm(ps[:], [(Wb[wi][:, jl, i4 * 128:(i4 + 1) * 128], aT[:, jl, :]) for jl in range(nj)], b_Wb[wi] + b_aT[0:nj], [bp])
                        p.add(DVE, lambda e, ps=ps, i=i: e.tensor_tensor(out=z[:, i, :], in0=z[:, i, :], in1=ps[:], op=ALU.add), reads=[bp, b_z[i]], writes=[b_z[i]])

        for tt in range(NTT):
            t0 = tt * 512
            for kc in range(32):
                pass
            p.add(SP, lambda e, t0=t0: e.dma_start(out=xTc, in_=srcT_s[:, t0:t0 + 512].rearrange("(kc p) t -> p kc t", p=128)),
                  reads=[b_src], writes=b_x, dma=True)
            p.add(SP, lambda e, t0=t0: e.dma_start(out=z, in_=zres_s[:, t0:t0 + 512].rearrange("(kc p) t -> p kc t", p=128)),
                  reads=[b_scr["zres"]], writes=b_z, dma=True)
            proj_res(win_s, b_win)
            layer_norm(layer * 2, False, t0)
            ffn(layer)
            layer_norm(layer * 2 + 1, final, t0)
            if not final:
                p.add(SP, lambda e, t0=t0: e.dma_start(out=zres_s[:, t0:t0 + 512].rearrange("(kc p) t -> p kc t", p=128), in_=z),
                      reads=b_z, writes=[b_scr["zres"]], dma=True)
                for ip in range(20):
                    wi = cc["wa"] % NWA; cc["wa"] += 1
                    load(Wa[wi], wcin_s[:, ip * 256:(ip + 1) * 256].rearrange("(kc p) n -> p kc n", p=128), b_wcin_all, b_Wa[wi])
                    for i2 in range(2):
                        i = ip * 2 + i2
                        ps, bp = bank()
                        mm(ps[:], [(Wa[wi][:, kc, i2 * 128:(i2 + 1) * 128], xTc[:, kc, :]) for kc in range(32)], b_Wa[wi] + b_x, [bp])
                        qi = cc["q"] % 3; cc["q"] += 1
                        p.add(ACT, lambda e, ps=ps, qi=qi: e.activation(out=qst[qi], in_=ps[:], func=AF.Copy), reads=[bp], writes=[b_qst[qi]])
                        store(qT_s[i * 128:(i + 1) * 128, t0:t0 + 512], qst[qi], [b_qst[qi]], [b_scr["qT"]])
                load(Wv, wcin_s[:, 5120:5632].rearrange("(kc p) n -> p kc n", p=128), b_wcin_all, b_Wv)
                for ts in range(4):
                    ps, bp = bank()
                    mm(ps[:], [(xTc[:, kc, ts * 128:(ts + 1) * 128], Wv[:, kc, :]) for kc in range(32)], b_Wv + b_x, [bp])
                    vi = cc["v"] % 2; cc["v"] += 1
                    vv4 = vst[vi].rearrange("p (g u d) -> p g u d", g=8, u=2, d=64)
                    pv = ps[:].rearrange("p (g d) -> p g d", g=8, d=64)
                    p.add(ACT, lambda e, vv4=vv4, pv=pv: e.activation(out=vv4[:, :, 0, :], in_=pv, func=AF.Copy), reads=[bp], writes=[b_vst[vi]])
                    p.add(DVE, lambda e, vv4=vv4, pv=pv: e.tensor_copy(out=vv4[:, :, 1, :], in_=pv), reads=[bp], writes=[b_vst[vi]])
                    store(v2_s[t0 + ts * 128:t0 + (ts + 1) * 128, :], vst[vi], [b_vst[vi]], [b_scr["v2"]])
        p.barrier()

    dense_chain(0, mixT_s, b_scr["mixT"], wabout_s, b_wabout, False)
    ck(4)

    ar.reset(BASE)
    BM = ar.f32(3, 64 * 128); b_BM = Buf()
    madd = ar.f32(3, 128); vflag = ar.f32(NCH * 3); esr = ar.bf16(64 * 128); sk = ar.f32(64)
    b_c4 = Buf()
    for kb in range(3):
        load(BM[:, kb, :], bm_d[:, kb, :], [], [b_BM])
    load(madd, maskadd_d, [], [b_c4]); load(vflag, vflag_d, [], [b_c4]); load(sk[0:1], sinks_d, [], [b_c4])
    p.add(ACT, lambda e: e.activation(out=sk[0:1], in_=sk[0:1], func=AF.Exp), reads=[b_c4], writes=[b_c4])
    for h in range(64):
        p.add(DVE, lambda e, h=h: e.tensor_scalar(out=esr[0:1, h * 128:(h + 1) * 128], in0=ones_f[0:1, :], scalar1=sk[0:1, h:h + 1], scalar2=None, op0=ALU.mult),
              reads=[b_c4, b_c1], writes=[b_c4])
        for kb in range(3):
            p.add(POOL if h % 2 else DVE, lambda e, h=h, kb=kb: e.tensor_tensor(out=BM[:, kb, h * 128:(h + 1) * 128], in0=BM[:, kb, h * 128:(h + 1) * 128],
                                                                              in1=madd[:, kb, :], op=ALU.add), reads=[b_BM, b_c4], writes=[b_BM])
    QTb = [ar.bf16(32, 128) for _ in range(2)]; KTb = [ar.bf16(8, 384) for _ in range(2)]; Vb = [ar.bf16(3, 1024) for _ in range(2)]
    b_at = [bufs(7) for _ in range(2)]
    tS = [ar.f32(512) for _ in range(3)]; b_tS = bufs(3)
    PT = [ar.bf16(3, 512) for _ in range(2)]; b_PT = [bufs(3) for _ in range(2)]
    rec = [ar.f32(512) for _ in range(2)]; b_rec = bufs(2)
    oTt = [ar.bf16(4, 128) for _ in range(2)]; b_oTt = bufs(2)
    ac = {"in": 0, "t": 0, "p": 0, "r": 0, "o": 0}
    for n in range(NCH):
        ii = ac["in"] % 2; ac["in"] += 1
        cols = slice(n * 128, (n + 1) * 128)
        load(QTb[ii], qT_s[0:4096, cols].rearrange("(a p) t -> p a t", p=128), [b_scr["qT"]], [b_at[ii][0]])
        for kb in range(3):
            nb_ = min(max(n - 1 + kb, 0), NCH - 1)
            load(KTb[ii][:, :, kb * 128:(kb + 1) * 128], qT_s[4096:5120, nb_ * 128:(nb_ + 1) * 128].rearrange("(g p) t -> p g t", p=128),
                 [b_scr["qT"]], [b_at[ii][1 + kb]])
            load(Vb[ii][:, kb, :], v2_s[nb_ * 128:(nb_ + 1) * 128, :], [b_scr["v2"]], [b_at[ii][4 + kb]])
        for g in range(8):
            oi = ac["o"] % 2; ac["o"] += 1
            for a in range(2):
                pa = slice(64 * a, 64 * a + 64)
                pi = ac["p"] % 2; ac["p"] += 1
                hq = (g * 2 + a) * 4
                for kb in range(3):
                    ps, bp = bank()
                    mm(ps[:], [(KTb[ii][pa, g, kb * 128:(kb + 1) * 128], QTb[ii][pa, 4 * g:4 * g + 4, :])], [b_at[ii][0], b_at[ii][1 + kb]], [bp])
                    ti = ac["t"] % 3; ac["t"] += 1
                    p.add(DVE, lambda e, ps=ps, ti=ti, kb=kb, hq=hq: e.scalar_tensor_tensor(out=tS[ti], in0=ps[:], scalar=0.125, in1=BM[:, kb, hq * 128:(hq + 4) * 128],
                                                                                       op0=ALU.mult, op1=ALU.add), reads=[bp, b_BM], writes=[b_tS[ti]])
                    p.add(ACT, lambda e, ti=ti, pi=pi, kb=kb, n=n: e.activation(out=PT[pi][:, kb, :], in_=tS[ti], func=AF.Exp, bias=vflag[:, n * 3 + kb:n * 3 + kb + 1]),
                          reads=[b_tS[ti], b_c4], writes=[b_PT[pi][kb]])
                pso, bpo = bank()
                mm(pso[:], [(Vb[ii][:, kb, g * 128:(g + 1) * 128], PT[pi][:, kb, :]) for kb in range(3)], b_at[ii][4:7] + b_PT[pi], [bpo])
                psd, bpd = bank()
                mm(psd[:], [(ones_bf, PT[pi][:, kb, :]) for kb in range(3)] + [(ones_bf[0:1, :], esr[0:1, hq * 128:(hq + 4) * 128])],
                   [b_const, b_c4] + b_PT[pi], [bpd])
                ri = ac["r"] % 2; ac["r"] += 1
                p.add(DVE, lambda e, psd=psd, ri=ri, pa=pa: e.reciprocal(out=rec[ri][pa, :], in_=psd[pa, :]), reads=[bpd], writes=[b_rec[ri]])
                p.add(DVE, lambda e, pso=pso, ri=ri, pa=pa, oi=oi: e.tensor_tensor(out=oTt[oi][pa].rearrange("p a b -> p (a b)"), in0=pso[pa, :], in1=rec[ri][pa, :], op=ALU.mult),
                      reads=[bpo, b_rec[ri]], writes=[b_oTt[oi]])
            store(oT_s[g * 512:(g + 1) * 512, cols].rearrange("(a p) t -> p a t", p=128), oTt[oi], [b_oTt[oi]], [b_scr["oT"]])
    p.barrier()
    ck(5)

    dense_chain(1, oT_s, b_scr["oT"], wcout_s, b_wcout, True)
    p.emit(nc)
    return nc


def _t5_bucket(rel):
    nb = 16
    max_exact = 8
    ret = np.where(rel > 0, nb, 0)
    n = np.abs(rel)
    nf = np.maximum(n, 1).astype(np.float32)
    large = max_exact + (np.log(nf / max_exact) / math.log(128 / max_exact) * (nb - max_exact)).astype(np.int32)
    large = np.minimum(large, nb - 1)
    return ret + np.where(n < max_exact, n, large)


def _consts(NT, seq_lens):
    bf = ml_dtypes.bfloat16
    NCH = NT // 128
    c = {}
    dc = np.zeros((NT, NT), np.float32)
    ds = np.zeros((NT, NT), np.float32)
    o = 0
    for S in seq_lens:
        k = np.arange(S, dtype=np.int64)
        ang = 2 * np.pi * ((k[:, None] * k[None, :]) % S) / S
        dc[o:o + S, o:o + S] = np.cos(ang) / np.sqrt(S)
        ds[o:o + S, o:o + S] = np.sin(ang) / np.sqrt(S)
        o += S
    c["dftc"] = dc.astype(bf)
    c["dfts"] = ds.astype(bf)
    k = np.arange(512, dtype=np.int64)
    ang = 2 * np.pi * ((k[:, None] * k[None, :]) % 512) / 512
    F = np.concatenate([np.cos(ang), -np.sin(ang)], axis=1) / np.sqrt(512)
    c["fch"] = np.ascontiguousarray(F.reshape(4, 128, 1024).transpose(1, 0, 2)).astype(bf)
    c["ident"] = np.eye(128, dtype=np.float32)
    c["identb"] = np.eye(128).astype(bf)
    j = np.arange(128)[:, None]
    i = np.arange(128)[None, :]
    s = -1.0 / 16
    glam = np.stack([(j <= i) * s, (j > i) * s, (j >= i) * s, (j < i) * s], axis=1).astype(np.float32)
    c["glam"] = np.ascontiguousarray(glam)
    c["smask"] = np.ascontiguousarray(np.stack([(j <= i), (j >= i)], axis=1).astype(np.float32))
    starts = set()
    ends = set()
    o = 0
    for S in seq_lens:
        starts.add(o // 128)
        o += S
        ends.add(o // 128 - 1)
    keep = np.ones((2, NCH + 1), np.float32)
    for cch in range(NCH):
        if cch in starts:
            keep[0, cch] = 0.0
        if cch in ends:
            keep[1, cch] = 0.0
    c["keep"] = np.ascontiguousarray(np.broadcast_to(keep.reshape(1, -1), (128, 2 * (NCH + 1)))).astype(np.float32)
    vf = np.zeros((NCH, 3), np.float32)
    for n in range(NCH):
        if n in starts:
            vf[n, 0] = NEG
        if n in ends:
            vf[n, 2] = NEG
    c["vflag"] = np.ascontiguousarray(np.broadcast_to(vf.reshape(1, -1), (128, NCH * 3))).astype(np.float32)
    pk = np.arange(128)[:, None]
    q = np.arange(128)[None, :]
    madd = np.zeros((128, 3, 128), np.float32)
    for kb in range(3):
        rel = (kb - 1) * 128 + pk - q
        madd[:, kb, :] = np.where(np.abs(rel) <= 128, 0.0, NEG)
    c["maskadd"] = madd
    return c


def _head_order():
    return np.array([8 * g + 2 * jj + a for g in range(8) for a in range(2) for jj in range(4)], dtype=np.int64)


def _per_core_inputs(NT, xin, seq_lens, W):
    c = _consts(NT, seq_lens)
    m = dict(c)
    m["x"] = np.ascontiguousarray(xin, dtype=np.float32)
    m.update(W)
    return m


def _shared_weight_inputs(rel_bias_table, ab_w_in, ab_fourier_g, ab_gate_w2, ab_gate_b, ab_head_norm_g, ab_w_out,
                          c_w_in, c_sinks, c_w_out, ffn_w1, ffn_w3, ffn_w2, ln_g, ln_b):
    W = {}
    W["ab_w_in"] = np.ascontiguousarray(ab_w_in[0], dtype=np.float32)
    W["ab_w_out"] = np.ascontiguousarray(ab_w_out[0], dtype=np.float32)
    W["c_w_in"] = np.ascontiguousarray(c_w_in[0], dtype=np.float32)
    W["c_w_out"] = np.ascontiguousarray(c_w_out[0], dtype=np.float32)
    W["ffn_w1"] = np.ascontiguousarray(ffn_w1, dtype=np.float32)
    W["ffn_w3"] = np.ascontiguousarray(ffn_w3, dtype=np.float32)
    W["ffn_w2"] = np.ascontiguousarray(ffn_w2, dtype=np.float32)
    W["fourier_g"] = np.ascontiguousarray(ab_fourier_g[0], dtype=np.float32)
    W["head_g"] = np.ascontiguousarray(ab_head_norm_g[0], dtype=np.float32)
    W["gate_w2"] = np.ascontiguousarray(np.transpose(ab_gate_w2[0], (1, 0, 2)), dtype=np.float32)
    W["gate_b"] = np.ascontiguousarray(ab_gate_b[0].reshape(1, 2, 1024), dtype=np.float32)
    lg = np.asarray(ln_g, np.float32).reshape(4, 32, 128).transpose(2, 0, 1)
    lb = np.asarray(ln_b, np.float32).reshape(4, 32, 128).transpose(2, 0, 1)
    W["lng"] = np.ascontiguousarray(lg)
    W["lnb"] = np.ascontiguousarray(lb)
    ho = _head_order()
    W["sinks"] = np.ascontiguousarray(np.asarray(c_sinks[0], np.float32)[ho].reshape(1, 64))
    pk = np.arange(128)[:, None]
    q = np.arange(128)[None, :]
    tab = np.asarray(rel_bias_table, np.float32)
    bm = np.empty((128, 3, 64, 128), np.float32)
    for kb in range(3):
        rel = (kb - 1) * 128 + pk - q
        bk = _t5_bucket(rel)
        gathered = tab[bk]
        bm[:, kb] = np.transpose(gathered[:, :, ho], (0, 2, 1))
    W["bm"] = np.ascontiguousarray(bm.reshape(128, 3, 64 * 128))
    return W


_NC_CACHE = {}


def _run(NT, core_x, core_seqs, W):
    if NT not in _NC_CACHE:
        _NC_CACHE[NT] = build(NT)
    nc = _NC_CACHE[NT]
    in_maps = [_per_core_inputs(NT, core_x[c], core_seqs[c], W) for c in range(8)]
    res = run_bass_kernel_spmd(nc, in_maps, core_ids=list(range(8)))
    return [np.asarray(r["y"]) for r in res.results]


def kernel(x_prompt, x_sample, rel_bias_table, ab_w_in, ab_fourier_g, ab_gate_w2, ab_gate_b, ab_head_norm_g, ab_w_out,
           c_w_in, c_sinks, c_w_out, ffn_w1, ffn_w3, ffn_w2, ln_g, ln_b):
    x_prompt = np.asarray(x_prompt, np.float32)
    x_sample = np.asarray(x_sample, np.float32)
    B, S, _ = x_prompt.shape
    DB, DS, _ = x_sample.shape
    assert B == 8 and DB == 4 and 2 * S == DS
    NT = DS
    W = _shared_weight_inputs(rel_bias_table, ab_w_in, ab_fourier_g, ab_gate_w2, ab_gate_b, ab_head_norm_g, ab_w_out,
                              c_w_in, c_sinks, c_w_out, ffn_w1, ffn_w3, ffn_w2, ln_g, ln_b)
    core_x = []
    core_seqs = []
    for c in range(4):
        core_x.append(x_prompt[2 * c:2 * c + 2].reshape(NT, D))
        core_seqs.append([S, S])
    for c in range(4):
        core_x.append(x_sample[c])
        core_seqs.append([DS])
    ys = _run(NT, core_x, core_seqs, W)
    y_prompt = np.stack([ys[c].reshape(2, S, D) for c in range(4)]).reshape(B, S, D).astype(np.float32)
    y_sample = np.stack([ys[4 + c] for c in range(4)]).astype(np.float32)
    return (y_prompt, y_sample)
```
